# Optimizing a Trainium2 kernel written in Bass

```python
import math
import jax, jax.numpy as jnp
from jax import lax
import numpy as np

D_MODEL = 1024
BATCH = 8
SEQ = 4096
DEPTH = 4

MIX_WIDTH = D_MODEL
HEAD_DIM = 64
SGU_WIDTH = MIX_WIDTH // 4
SGU_GROUPS = SGU_WIDTH // HEAD_DIM
SGU_CHUNK = 128
SSM_INNER = MIX_WIDTH // 2
SSM_HEAD_DIM = 64
SSM_HEADS = SSM_INNER // SSM_HEAD_DIM
SSM_GROUPS = 2
SSM_STATE = 128
SSM_CONV = 4
SSM_CHUNK = 128
SSM_CONV_CH = SSM_INNER + 2 * SSM_GROUPS * SSM_STATE
ATTN_WIDTH = MIX_WIDTH - SGU_WIDTH - SSM_INNER
ATTN_HEADS = ATTN_WIDTH // HEAD_DIM
DILATED_PATTERNS = ((128, 1), (512, 4), (2048, 16))
ATTN_BLOCK = 128
REL_BUCKETS = 32
REL_MAX_DIST = 2048
MEM_LEN = 256
XATTN_HEADS = 4
XATTN_HEAD_DIM = D_MODEL // XATTN_HEADS
D_FF = 2816
EPS = 1e-6

OFF_SGU = 0
OFF_Z = OFF_SGU + 2 * SGU_WIDTH
OFF_XBC = OFF_Z + SSM_INNER
OFF_DT = OFF_XBC + SSM_CONV_CH
OFF_QKV = OFF_DT + SSM_HEADS
N_IN = OFF_QKV + 3 * ATTN_WIDTH

kernel_name = "hybrid_sgu_ssd_dilated_macaron_trunk"


def rms_norm(x, g):
    xf = x.astype(jnp.float32)
    y = xf * lax.rsqrt(jnp.mean(xf * xf, axis=-1, keepdims=True) + EPS)
    return (y * g.astype(jnp.float32)).astype(x.dtype)


def swiglu(x, wi, wo):
    g, u = jnp.split(x @ wi, 2, axis=-1)
    return (jax.nn.silu(g) * u) @ wo


def t5_bucket(dist):
    max_exact = REL_BUCKETS // 2
    d = jnp.maximum(dist, 1).astype(jnp.float32)
    large = max_exact + (jnp.log(d / max_exact) / math.log(REL_MAX_DIST / max_exact)
                         * (REL_BUCKETS - max_exact)).astype(jnp.int32)
    large = jnp.minimum(large, REL_BUCKETS - 1)
    return jnp.where(dist < max_exact, dist, large)


def spatial_gating(uv, ln_g, w_s, b_s):
    B_, S_, _ = uv.shape
    u, v = jnp.split(jax.nn.gelu(uv).astype(jnp.float32), 2, axis=-1)
    mu = jnp.mean(v, axis=-1, keepdims=True)
    var = jnp.mean(jnp.square(v - mu), axis=-1, keepdims=True)
    vn = (v - mu) * lax.rsqrt(var + EPS) * ln_g.astype(jnp.float32)
    nc = S_ // SGU_CHUNK
    vn = vn.reshape(B_, nc, SGU_CHUNK, SGU_GROUPS, HEAD_DIM)
    causal = jnp.tril(jnp.ones((SGU_CHUNK, SGU_CHUNK), bool))
    w = jnp.where(causal[None], w_s.astype(jnp.float32), 0.0)
    s = jnp.einsum('gts,bcsgd->bctgd', w, vn) + b_s.astype(jnp.float32).T[None, None, :, :, None]
    return (u * s.reshape(B_, S_, SGU_WIDTH)).astype(uv.dtype)


def ssd_scan(xs, dt, A, Bm, Cm):
    B_, S_, H_, P_ = xs.shape
    G_, N_ = Bm.shape[2], Bm.shape[3]
    E_ = H_ // G_
    T = SSM_CHUNK
    nc = S_ // T
    x = (xs * dt[..., None]).reshape(B_, nc, T, G_, E_, P_)
    a = (dt * A).reshape(B_, nc, T, G_, E_)
    Bc = Bm.reshape(B_, nc, T, G_, N_)
    Cc = Cm.reshape(B_, nc, T, G_, N_)
    acs = jnp.cumsum(a, axis=2)
    causal = jnp.tril(jnp.ones((T, T), bool))[:, :, None, None]
    seg = acs[:, :, :, None] - acs[:, :, None, :]
    decay_ls = jnp.exp(jnp.where(causal, seg, -jnp.inf))
    cb = jnp.einsum('bclgn,bcsgn->bclsg', Cc, Bc)
    y_diag = jnp.einsum('bclsge,bcsgep->bclgep', cb[..., None] * decay_ls, x)
    decay_to_end = jnp.exp(acs[:, :, -1:] - acs)
    states = jnp.einsum('bcsgn,bcsge,bcsgep->bcgepn', Bc, decay_to_end, x)
    chunk_decay = jnp.exp(acs[:, :, -1])

    def step(h, inp):
        dec, st = inp
        return h * dec[..., None, None] + st, h

    h0 = jnp.zeros((B_, G_, E_, P_, N_), jnp.float32)
    _, h_in = lax.scan(step, h0, (jnp.moveaxis(chunk_decay, 1, 0), jnp.moveaxis(states, 1, 0)))
    h_in = jnp.moveaxis(h_in, 0, 1)
    y_off = jnp.einsum('bclgn,bcgepn->bclgep', Cc, h_in) * jnp.exp(acs)[..., None]
    return (y_diag + y_off).reshape(B_, S_, H_, P_)


def ssd_mixer(z, xbc, dt_raw, conv_w, conv_b, dt_bias, a_log, d_skip, norm_g):
    B_, S_, _ = xbc.shape
    xbc = lax.conv_general_dilated(xbc, conv_w.astype(xbc.dtype)[:, None, :], window_strides=(1,),
                                   padding=[(SSM_CONV - 1, 0)], dimension_numbers=('NWC', 'WIO', 'NWC'),
                                   feature_group_count=SSM_CONV_CH) + conv_b
    xbc = jax.nn.silu(xbc).astype(jnp.float32)
    xs = xbc[..., :SSM_INNER].reshape(B_, S_, SSM_HEADS, SSM_HEAD_DIM)
    Bm = xbc[..., SSM_INNER:SSM_INNER + SSM_GROUPS * SSM_STATE].reshape(B_, S_, SSM_GROUPS, SSM_STATE)
    Cm = xbc[..., SSM_INNER + SSM_GROUPS * SSM_STATE:].reshape(B_, S_, SSM_GROUPS, SSM_STATE)
    dt = jax.nn.softplus(dt_raw.astype(jnp.float32) + dt_bias.astype(jnp.float32))
    A = -jnp.exp(a_log.astype(jnp.float32))
    y = ssd_scan(xs, dt, A, Bm, Cm) + d_skip.astype(jnp.float32)[:, None] * xs
    y = y.reshape(B_, S_, SSM_INNER) * jax.nn.silu(z.astype(jnp.float32))
    yg = y.reshape(B_, S_, SSM_GROUPS, SSM_INNER // SSM_GROUPS)
    yg = yg * lax.rsqrt(jnp.mean(yg * yg, axis=-1, keepdims=True) + EPS)
    return (yg.reshape(B_, S_, SSM_INNER) * norm_g.astype(jnp.float32)).astype(z.dtype)


def dilated_branch(q, k, v, rel_bias, window, dil):
    B_, S_, H_, E_ = q.shape
    T = ATTN_BLOCK
    span = window // dil
    L = S_ // dil
    nb = -(-L // T)
    Lp = nb * T

    def residues(t, front):
        t = t.reshape(B_, L, dil, H_, E_)
        return jnp.pad(t, ((0, 0), (front, Lp - L), (0, 0), (0, 0), (0, 0)))

    qb = residues(q, 0).reshape(B_, nb, T, dil, H_, E_)
    kb = residues(k, T).reshape(B_, nb + 1, T, dil, H_, E_)
    vb = residues(v, T).reshape(B_, nb + 1, T, dil, H_, E_)
    kw = jnp.concatenate([kb[:, :-1], kb[:, 1:]], axis=2)
    vw = jnp.concatenate([vb[:, :-1], vb[:, 1:]], axis=2)
    dist = jnp.arange(T)[:, None] + T - jnp.arange(2 * T)[None, :]
    band = (dist >= 0) & (dist <= span)
    key_idx = jnp.arange(nb)[:, None] * T - T + jnp.arange(2 * T)[None, :]
    mask = band[None] & (key_idx >= 0)[:, None, :]
    bias = jnp.transpose(rel_bias[t5_bucket(jnp.maximum(dist, 0) * dil)], (2, 0, 1))
    logits = jnp.einsum('bnirhe,bnjrhe->bnrhij', qb, kw).astype(jnp.float32) * (E_ ** -0.5)
    logits = logits + bias.astype(jnp.float32)
    logits = jnp.where(mask[None, :, None, None], logits, -jnp.inf)
    m = jnp.max(logits, axis=-1)
    p = jnp.exp(logits - m[..., None])
    s = jnp.sum(p, axis=-1)
    o = jnp.einsum('bnrhij,bnjrhe->bnirhe', p, vw.astype(jnp.float32))
    o = o / jnp.transpose(s, (0, 1, 4, 2, 3))[..., None]
    lse = jnp.transpose(m + jnp.log(s), (0, 1, 4, 2, 3))
    o = o.reshape(B_, Lp, dil, H_, E_)[:, :L].reshape(B_, S_, H_, E_)
    lse = lse.reshape(B_, Lp, dil, H_)[:, :L].reshape(B_, S_, H_)
    return o, lse


def dilated_attention(q, k, v, rel_bias):
    outs, lses = [], []
    for window, dil in DILATED_PATTERNS:
        o, lse = dilated_branch(q, k, v, rel_bias, window, dil)
        outs.append(o)
        lses.append(lse)
    w = jax.nn.softmax(jnp.stack(lses), axis=0)
    return jnp.einsum('kbsh,kbshe->bshe', w, jnp.stack(outs)).astype(q.dtype)


def token_mixer(hn, w_in, w_out, sgu_ln_g, sgu_w, sgu_b, conv_w, conv_b, dt_bias, a_log, d_skip,
                ssm_norm_g, rel_bias):
    B_, S_, _ = hn.shape
    proj = hn @ w_in
    a_out = spatial_gating(proj[..., OFF_SGU:OFF_Z], sgu_ln_g, sgu_w, sgu_b)
    b_out = ssd_mixer(proj[..., OFF_Z:OFF_XBC], proj[..., OFF_XBC:OFF_DT], proj[..., OFF_DT:OFF_QKV],
                      conv_w, conv_b, dt_bias, a_log, d_skip, ssm_norm_g)
    q, k, v = jnp.split(proj[..., OFF_QKV:].reshape(B_, S_, 3, ATTN_HEADS, HEAD_DIM), 3, axis=2)
    c_out = dilated_attention(q[:, :, 0], k[:, :, 0], v[:, :, 0], rel_bias).reshape(B_, S_, ATTN_WIDTH)
    mixed = jnp.concatenate([a_out.astype(hn.dtype), b_out.astype(hn.dtype), c_out], axis=-1)
    return mixed @ w_out


def memory_cross_attention(hn, mem_n, wq, wkv, wo):
    B_, S_, _ = hn.shape
    M_ = mem_n.shape[1]
    q = (hn @ wq).reshape(B_, S_, XATTN_HEADS, XATTN_HEAD_DIM)
    k, v = jnp.split((mem_n @ wkv).reshape(B_, M_, 2, XATTN_HEADS, XATTN_HEAD_DIM), 2, axis=2)
    logits = jnp.einsum('bshe,bmhe->bhsm', q, k[:, :, 0]).astype(jnp.float32) * (XATTN_HEAD_DIM ** -0.5)
    p = jax.nn.softmax(logits, axis=-1)
    o = jnp.einsum('bhsm,bmhe->bshe', p, v[:, :, 0].astype(jnp.float32)).astype(hn.dtype)
    return o.reshape(B_, S_, D_MODEL) @ wo


def setup_inputs(seed: int = 0) -> dict:
    key = jax.random.key(seed)
    ks = jax.random.split(key, 24)
    f32 = jnp.float32

    def nrm(k, shape, scale):
        return jax.random.normal(k, shape, f32) * scale

    dt0 = jnp.exp(jax.random.uniform(ks[13], (DEPTH, SSM_HEADS), f32, math.log(1e-3), math.log(1e-1)))
    return {
        'x': nrm(ks[0], (BATCH, SEQ, D_MODEL), 1.0),
        'mem': nrm(ks[1], (BATCH, MEM_LEN, D_MODEL), 1.0),
        'norm_pre': 1.0 + nrm(ks[2], (DEPTH, 4, D_MODEL), 0.05),
        'norm_post': 1.0 + nrm(ks[3], (DEPTH, 4, D_MODEL), 0.05),
        'ffn_wi': nrm(ks[4], (DEPTH, 2, D_MODEL, 2 * D_FF), D_MODEL ** -0.5),
        'ffn_wo': nrm(ks[5], (DEPTH, 2, D_FF, D_MODEL), D_FF ** -0.5),
        'mix_w_in': nrm(ks[6], (DEPTH, D_MODEL, N_IN), D_MODEL ** -0.5),
        'mix_w_out': nrm(ks[7], (DEPTH, MIX_WIDTH, D_MODEL), MIX_WIDTH ** -0.5),
        'sgu_ln_g': 1.0 + nrm(ks[8], (DEPTH, SGU_WIDTH), 0.05),
        'sgu_w': nrm(ks[9], (DEPTH, SGU_GROUPS, SGU_CHUNK, SGU_CHUNK), SGU_CHUNK ** -0.5),
        'sgu_b': 1.0 + nrm(ks[10], (DEPTH, SGU_GROUPS, SGU_CHUNK), 0.1),
        'ssm_conv_w': nrm(ks[11], (DEPTH, SSM_CONV, SSM_CONV_CH), SSM_CONV ** -0.5),
        'ssm_conv_b': nrm(ks[12], (DEPTH, SSM_CONV_CH), 0.02),
        'ssm_dt_bias': dt0 + jnp.log(-jnp.expm1(-dt0)),
        'ssm_a_log': jnp.log(jax.random.uniform(ks[14], (DEPTH, SSM_HEADS), f32, 1.0, 16.0)),
        'ssm_d': 1.0 + nrm(ks[15], (DEPTH, SSM_HEADS), 0.1),
        'ssm_norm_g': 1.0 + nrm(ks[16], (DEPTH, SSM_INNER), 0.05),
        'rel_bias': nrm(ks[17], (REL_BUCKETS, ATTN_HEADS), 0.5),
        'mem_norm_g': 1.0 + nrm(ks[18], (DEPTH, D_MODEL), 0.05),
        'xattn_wq': nrm(ks[19], (DEPTH, D_MODEL, D_MODEL), D_MODEL ** -0.5),
        'xattn_wkv': nrm(ks[20], (DEPTH, D_MODEL, 2 * D_MODEL), D_MODEL ** -0.5),
        'xattn_wo': nrm(ks[21], (DEPTH, D_MODEL, D_MODEL), D_MODEL ** -0.5),
    }


def reference(x, mem, norm_pre, norm_post, ffn_wi, ffn_wo, mix_w_in, mix_w_out, sgu_ln_g, sgu_w, sgu_b,
              ssm_conv_w, ssm_conv_b, ssm_dt_bias, ssm_a_log, ssm_d, ssm_norm_g, rel_bias, mem_norm_g,
              xattn_wq, xattn_wkv, xattn_wo):
    h = x
    for l in range(DEPTH):
        hn = rms_norm(h, norm_pre[l, 0])
        h = h + 0.5 * rms_norm(swiglu(hn, ffn_wi[l, 0], ffn_wo[l, 0]), norm_post[l, 0])
        hn = rms_norm(h, norm_pre[l, 1])
        mixed = token_mixer(hn, mix_w_in[l], mix_w_out[l], sgu_ln_g[l], sgu_w[l], sgu_b[l], ssm_conv_w[l],
                            ssm_conv_b[l], ssm_dt_bias[l], ssm_a_log[l], ssm_d[l], ssm_norm_g[l], rel_bias)
        h = h + rms_norm(mixed, norm_post[l, 1])
        hn = rms_norm(h, norm_pre[l, 2])
        mem_n = rms_norm(mem, mem_norm_g[l])
        h = h + rms_norm(memory_cross_attention(hn, mem_n, xattn_wq[l], xattn_wkv[l], xattn_wo[l]), norm_post[l, 2])
        hn = rms_norm(h, norm_pre[l, 3])
        h = h + 0.5 * rms_norm(swiglu(hn, ffn_wi[l, 1], ffn_wo[l, 1]), norm_post[l, 3])
    return h
```

```python
import math
import numpy as np
import ml_dtypes
import concourse.bass as bass
import concourse.mybir as mybir
from concourse.bass_utils import run_bass_kernel_spmd

F32 = mybir.dt.float32
BF16 = mybir.dt.bfloat16
AF = mybir.ActivationFunctionType
ALU = mybir.AluOpType

ENGS = ("pe", "act", "dve", "pool", "sp")

D = 1024
S = 4096
DEPTH = 4
DFF = 2816
NIN = 2824
MEM = 256
EPS = 1e-6
TT = 512
NT = S // TT
NSLAB = 6


class Buf:
    __slots__ = ("name", "t", "last_w", "readers", "dsem", "dcount")

    def __init__(self, name, t=None):
        self.name = name
        self.t = t
        self.last_w = None
        self.readers = []
        self.dsem = None
        self.dcount = 0

    def __getitem__(self, key):
        return self.t[key]


class Op:
    __slots__ = ("eng", "fn", "reads", "writes", "dma", "deps", "idx", "sig", "dbuf", "dval", "extra")

    def __init__(self, eng, fn, reads, writes, dma=False):
        self.eng = eng
        self.fn = fn
        self.reads = reads
        self.writes = writes
        self.dma = dma
        self.deps = []
        self.sig = None
        self.dbuf = None
        self.dval = None
        self.extra = []


class Pool:
    def __init__(self, bufs):
        self.bufs = bufs
        self.i = 0

    def next(self):
        b = self.bufs[self.i % len(self.bufs)]
        self.i += 1
        return b


class Prog:
    def __init__(self, nc):
        self.nc = nc
        self.ops = []
        self._ctx = []
        self._nm = 0
        self.pending = {e: [] for e in ENGS}
        self.sbuf_dmas = []

    def sbuf(self, name, shape, dtype):
        self._nm += 1
        g = self.nc.sbuf_tensor(f"{name}_{self._nm}", list(shape), dtype)
        t = g.__enter__()
        b = Buf(name, t)
        self._ctx.append((g, b))
        return b

    def psum(self, name, shape, dtype=F32):
        self._nm += 1
        g = self.nc.psum_tensor(f"{name}_{self._nm}", list(shape), dtype)
        t = g.__enter__()
        b = Buf(name, t)
        self._ctx.append((g, b))
        return b

    def pool(self, name, shape, dtype, n, psum=False):
        f = self.psum if psum else self.sbuf
        return Pool([f(f"{name}{i}", shape, dtype) for i in range(n)])

    def key(self, name):
        return Buf(name, None)

    def mark(self):
        return len(self._ctx)

    def release(self, mark, final=False):
        rel = []
        while len(self._ctx) > mark:
            g, b = self._ctx.pop()
            g.__exit__(None, None, None)
            rel.append(b)
        if not final:
            o = Op(None, None, [], [], False)
            o.idx = len(self.ops)
            o.extra = rel
            self.ops.append(o)

    def op(self, eng, fn, reads=(), writes=(), dma=False):
        o = Op(eng, fn, list(reads), list(writes), dma)
        o.idx = len(self.ops)
        if self.pending[eng]:
            o.extra = self.pending[eng]
            self.pending[eng] = []
        self.ops.append(o)
        return o

    def dma(self, out, in_, reads=(), writes=(), q="sp", dbuf=None, **kw):
        o = self.op(q, lambda e: e.dma_start(out=out, in_=in_, **kw), reads, writes, dma=True)
        o.dbuf = dbuf
        if any(b.t is not None for b in o.reads + o.writes):
            self.sbuf_dmas.append(o)
        return o

    def barrier(self):
        last = {}
        for o in self.ops:
            if o.eng is not None and not o.dma:
                last[o.eng] = o
        deps = list(last.values()) + list(self.sbuf_dmas)
        self.sbuf_dmas = []
        for e in ENGS:
            self.pending[e] = self.pending[e] + deps

    def analyze(self, upto=None):
        pass

    def finish(self):
        nc = self.nc
        ops = self.ops
        for o in ops:
            if o.eng is None:
                continue
            deps = {}
            for b in o.reads:
                if b.last_w is not None:
                    deps[b.last_w.idx] = ("raw", b.last_w)
            for b in o.writes:
                if b.last_w is not None and b.last_w.idx not in deps:
                    deps[b.last_w.idx] = ("waw", b.last_w)
                for r in b.readers:
                    if r.idx not in deps:
                        deps[r.idx] = ("war", r)
            for b in o.reads:
                b.readers.append(o)
            for b in o.writes:
                b.last_w = o
                b.readers = []
            keep = []
            for kind, d in deps.values():
                if d is o:
                    continue
                if d.eng == o.eng and not d.dma and not o.dma:
                    if o.eng == "pe":
                        continue
                    if kind != "raw":
                        continue
                keep.append(d)
            seen = set(id(d) for d in keep)
            for d in o.extra:
                if id(d) in seen or d is o:
                    continue
                if d.eng == o.eng and not d.dma and not o.dma:
                    continue
                seen.add(id(d))
                keep.append(d)
            o.deps = keep
        need = set()
        for o in ops:
            if o.eng is None:
                continue
            if o.dma:
                need.add(o.idx)
            for d in o.deps:
                need.add(d.idx)
        cnt = {e: 0 for e in ENGS}
        sem_ctx = []

        def mksem(name):
            g = nc.semaphore(name)
            s = g.__enter__()
            sem_ctx.append(g)
            return s

        esem = {}
        nd = 0
        free_sems = []
        for o in ops:
            if o.eng is None:
                for b in o.extra:
                    if b.dsem is not None:
                        free_sems.append(b.dsem)
                continue
            if o.idx not in need:
                continue
            if o.dma:
                b = o.dbuf
                if b is None:
                    cands = [x for x in (o.writes + o.reads) if x.t is not None]
                    b = cands[0] if cands else (o.writes + o.reads)[0]
                if b.dsem is None:
                    if free_sems:
                        b.dsem = free_sems.pop()
                    else:
                        nd += 1
                        b.dsem = [mksem(f"d{nd}"), 0]
                b.dsem[1] += 16
                o.dbuf = b
                o.dval = b.dsem[1]
            else:
                if o.eng not in esem:
                    esem[o.eng] = mksem("e_" + o.eng)
                cnt[o.eng] += 1
                o.sig = cnt[o.eng]
        self.sig_counts = dict(cnt)
        self.n_dsem = nd
        for o in ops:
            if o.eng is not None and o.dma:
                o.dbuf = type("D", (), {"dsem": o.dbuf.dsem})()
        by_eng = {e: [o for o in ops if o.eng == e] for e in ENGS}
        with nc.Block() as block:
            def emit(engname, eng):
                waited = {}
                for o in by_eng[engname]:
                    for d in o.deps:
                        if d.dma:
                            k = ("d", id(d.dbuf.dsem))
                            sem, val = d.dbuf.dsem[0], d.dval
                        else:
                            k = ("e", d.eng)
                            sem, val = esem[d.eng], d.sig
                        if waited.get(k, 0) >= val:
                            continue
                        waited[k] = val
                        eng.wait_ge(sem, val)
                    ins = o.fn(eng)
                    if o.idx in need:
                        if o.dma:
                            ins.then_inc(o.dbuf.dsem[0], 16)
                        else:
                            ins.then_inc(esem[o.eng], 1)

            if by_eng["sp"]:
                block.sync(lambda e: emit("sp", e))
            if by_eng["act"]:
                block.scalar(lambda e: emit("act", e))
            if by_eng["dve"]:
                block.vector(lambda e: emit("dve", e))
            if by_eng["pool"]:
                block.gpsimd(lambda e: emit("pool", e))
            if by_eng["pe"]:
                block.tensor(lambda e: emit("pe", e))
        for g in reversed(sem_ctx):
            g.__exit__(None, None, None)
        self.release(0, final=True)


class Builder:
    def __init__(self, nlayers=DEPTH, plan=None, debug=False):
        self.debug = debug
        self.nl = nlayers
        self.plan = plan
        nc = bass.Bass("TRN2", target_bir_lowering=False)
        self.nc = nc
        self.P = Prog(nc)
        self.dram = {}
        self.allkeys = []

    def din(self, name, shape, dtype=F32):
        ap = self.nc.dram_tensor(name, list(shape), dtype, kind="ExternalInput").ap()
        self.dram[name] = ap
        return ap

    def dscr(self, name, shape, dtype):
        kind = "ExternalOutput" if (self.debug and name in ("a_s", "b_s", "c_s", "qkv_s")) else "Internal"
        ap = self.nc.dram_tensor(name, list(shape), dtype, kind=kind).ap()
        self.dram[name] = ap
        return ap

    def key(self, name):
        k = self.P.key(name)
        self.allkeys.append(k)
        return k

    def declare(self):
        nl = DEPTH
        self.x = self.din("x", [S, D])
        self.mem = self.din("mem", [MEM, D])
        self.norm_pre = self.din("norm_pre", [nl, 4, D])
        self.norm_post = self.din("norm_post", [nl, 4, D])
        self.ffn_wi = self.din("ffn_wi", [nl, 2, D, 2 * DFF])
        self.ffn_wo = self.din("ffn_wo", [nl, 2, DFF, D])
        self.mix_w_in = self.din("mix_w_in", [nl, D, NIN])
        self.mix_w_out = self.din("mix_w_out", [nl, D, D])
        self.sgu_ln_g = self.din("sgu_ln_g", [nl, 256])
        self.sgu_w = self.din("sgu_w", [nl, 4, 128, 128])
        self.sgu_b = self.din("sgu_b", [nl, 4, 128])
        self.conv_w = self.din("ssm_conv_w", [nl, 4, 1024])
        self.conv_b = self.din("ssm_conv_b", [nl, 1024])
        self.dt_bias = self.din("ssm_dt_bias", [nl, 8])
        self.a_log = self.din("ssm_a_log", [nl, 8])
        self.ssm_d = self.din("ssm_d", [nl, 8])
        self.ssm_norm_g = self.din("ssm_norm_g", [nl, 512])
        self.relb = self.din("relb", [3, 4, 128, 256])
        self.amask = self.din("amask", [128, 256])
        self.mem_norm_g = self.din("mem_norm_g", [nl, D])
        self.wq = self.din("xattn_wq", [nl, D, D])
        self.wkv = self.din("xattn_wkv", [nl, D, 2 * D])
        self.wo = self.din("xattn_wo", [nl, D, D])
        self.c_ident = self.din("c_ident", [128, 128])
        self.c_tri = self.din("c_tri", [3, 128, 128])
        self.out = self.nc.dram_tensor("out", [S, D], F32, kind="ExternalOutput").ap()
        self.h = self.dscr("h_scr", [S, D], F32)
        self.wi_s = self.dscr("wi_s", [nl, 2, NSLAB, 128, 8, 2, 512], BF16)
        self.wo_s = self.dscr("wo_s", [nl, 2, DFF, D], BF16)
        self.win_s = self.dscr("win_s", [nl, D, NIN], BF16)
        self.wout_s = self.dscr("wout_s", [nl, D, D], BF16)
        self.wq_s = self.dscr("wq_s", [nl, D, D], BF16)
        self.wkv_s = self.dscr("wkv_s", [nl, D, 2 * D], BF16)
        self.wox_s = self.dscr("wox_s", [nl, D, D], BF16)
        self.a_s = self.dscr("a_s", [4, 64, S], BF16)
        self.b_s = self.dscr("b_s", [4, 128, S], BF16)
        self.c_s = self.dscr("c_s", [4, 64, S], BF16)
        self.qkv_s = self.dscr("qkv_s", [6, 128, S], BF16)
        self.k_ab = self.key("ab")
        self.k_qkv = self.key("qkv")
        self.k_c = self.key("c")
        self.k_h = self.key("h")
        self.k_out = self.key("out")
        self.k_prep = self.P.key("prep")
        self.kprep = {}

    def prep_weights(self, l, which=("ffn", "mix", "xa")):
        P = self.P
        kp = self.k_prep
        grp = [None]

        def cast(out, in_):
            P.dma(out, in_, writes=[kp, grp[0]], q="pool", dbuf=kp)

        if "ffn" in which:
            for f in range(2):
                grp[0] = self.kprep[(l, "ffn%d" % f)] = P.key("prep_ffn")
                wi = self.ffn_wi[l, f].rearrange("(k p) n -> p k n", p=128)
                for jg in range(NSLAB):
                    w = 512 if jg < 5 else 256
                    for t in range(2):
                        c0 = t * DFF + jg * 512
                        cast(self.wi_s[l, f, jg, :, :, t, 0:w], wi[:, :, c0:c0 + w])
                for r0 in range(0, DFF, 704):
                    cast(self.wo_s[l, f, r0:r0 + 704, :], self.ffn_wo[l, f, r0:r0 + 704, :])
        if "mix" in which:
            grp[0] = self.kprep[(l, "mix")] = P.key("prep_mix")
            for r0 in range(0, D, 512):
                cast(self.win_s[l, r0:r0 + 512, :], self.mix_w_in[l, r0:r0 + 512, :])
            cast(self.wout_s[l], self.mix_w_out[l])
        if "xa" in which:
            grp[0] = self.kprep[(l, "xa")] = P.key("prep_xa")
            cast(self.wq_s[l], self.wq[l])
            for r0 in range(0, D, 512):
                cast(self.wkv_s[l, r0:r0 + 512, :], self.wkv[l, r0:r0 + 512, :])
            cast(self.wox_s[l], self.wo[l])

    def consts(self):
        P = self.P
        self.ident_f = P.sbuf("ident_f", [128, 128], F32)
        self.ident = P.sbuf("ident", [128, 128], BF16)
        P.dma(self.ident_f[:], self.c_ident[:, :], writes=[self.ident_f])
        P.op("dve", lambda e: e.tensor_copy(out=self.ident[:], in_=self.ident_f[:]), [self.ident_f], [self.ident])
        self.junk = P.sbuf("junk", [128, 1024], BF16)
        self.eps_t = P.sbuf("eps_t", [128, 1], F32)
        self.biasT = P.sbuf("biasT", [128, 12, 256], BF16)
        self.ones_m = P.sbuf("ones_m", [128, 128], BF16)
        self.negones = P.sbuf("negones", [128, 128], BF16)
        P.op("dve", lambda e: e.memset(self.ones_m[:], 1.0), [], [self.ones_m])
        P.op("dve", lambda e: e.memset(self.negones[:], -1.0), [], [self.negones])
        self.neg128 = P.sbuf("neg128", [128, 128], BF16)
        P.op("dve", lambda e: e.memset(self.neg128[:], -1.0 / 128.0), [], [self.neg128])
        m = P.mark()
        relb_f = P.sbuf("relb_f", [128, 12, 256], F32)
        am = P.sbuf("am", [128, 256], F32)
        P.dma(relb_f[:], self.relb.rearrange("p h j i -> j (p h) i"), writes=[relb_f])
        P.dma(am[:], self.amask[:, :], writes=[am])
        P.op("dve", lambda e: e.tensor_tensor(out=relb_f[:], in0=relb_f[:], in1=am[:].unsqueeze(1).to_broadcast([128, 12, 256]),
                                              op=ALU.add), [relb_f, am], [relb_f])
        P.op("dve", lambda e: e.tensor_copy(out=self.biasT[:], in_=relb_f[:]), [relb_f], [self.biasT])
        P.barrier()
        P.release(m)
        P.op("dve", lambda e: e.memset(self.eps_t[:], EPS), [], [self.eps_t])

    def load_gain(self, buf, src_row, scale=None):
        P = self.P
        P.dma(buf[:], src_row.partition_broadcast(128), writes=[buf])
        if scale is not None:
            P.op("dve", lambda e: e.tensor_scalar(out=buf[:], in0=buf[:], scalar1=float(scale), scalar2=None,
                                                  op0=ALU.mult), [buf], [buf])

    def front(self, src, t, hT, gpre, hn, hnT, ss, rstd, tp_pool, use_ln=False):
        self.front_a(src, t, hT, gpre, hn, ss, rstd, use_ln=use_ln)
        self.front_b(hn, hnT, tp_pool)

    def rsqrt_lnexp(self, out_ap, in_ap, scale, bufs_r, bufs_w):
        P = self.P
        P.op("act", lambda e: e.activation(out=out_ap, in_=in_ap, func=AF.Ln, scale=scale, bias=self.eps_t[:]),
             list(bufs_r) + [self.eps_t], list(bufs_w))
        P.op("act", lambda e: e.activation(out=out_ap, in_=out_ap, func=AF.Exp, scale=-0.5), list(bufs_w), list(bufs_w))

    def front_a(self, src, t, hT, gpre, hn, ss, rstd, use_ln=False):
        P = self.P
        junk = self.junk
        tile = src[t * TT:(t + 1) * TT, :].rearrange("(s p) d -> p s d", p=128)
        P.dma(hT[:], tile, reads=[self.k_h], writes=[hT])
        for s in range(4):
            P.op("act", lambda e, s=s: e.activation(out=junk[:], in_=hT[:, s, :], func=AF.Square,
                                                     accum_out=ss[:, s:s + 1]), [hT], [junk, ss])
        if use_ln:
            self.rsqrt_lnexp(rstd[:], ss[:], 1.0 / D, [ss], [rstd])
        else:
            P.op("act", lambda e: e.activation(out=rstd[:], in_=ss[:], func=AF.Sqrt, scale=1.0 / D, bias=self.eps_t[:]),
                 [ss, self.eps_t], [rstd])
            P.op("dve", lambda e: e.reciprocal(out=rstd[:], in_=rstd[:]), [rstd], [rstd])
        for s in range(4):
            P.op("dve", lambda e, s=s: e.scalar_tensor_tensor(out=hn[:, s, :], in0=hT[:, s, :], scalar=rstd[:, s:s + 1],
                                                               in1=gpre[:], op0=ALU.mult, op1=ALU.mult),
                 [hT, rstd, gpre], [hn])

    def front_b(self, hn, hnT, tp_pool):
        P = self.P
        ident = self.ident
        for c2 in range(4):
            tp = tp_pool.next()
            for cc in range(2):
                c = 2 * c2 + cc
                for s in range(4):
                    P.op("pe", lambda e, s=s, c=c, cc=cc, tp=tp: e.transpose(
                        tp[:, cc * 512 + s * 128:cc * 512 + (s + 1) * 128], hn[:, s, c * 128:(c + 1) * 128], ident[:]),
                        [hn, ident], [tp])
            src = tp[:].rearrange("p (c t) -> p c t", c=2)
            if c2 % 2 == 0:
                P.op("act", lambda e, c2=c2, src=src: e.activation(out=hnT[:, 2 * c2:2 * c2 + 2, :], in_=src, func=AF.Copy),
                     [tp], [hnT])
            else:
                P.op("dve", lambda e, c2=c2, src=src: e.tensor_copy(out=hnT[:, 2 * c2:2 * c2 + 2, :], in_=src), [tp], [hnT])

    def back(self, dst, t, hT, actT, nk, w_rhs, w_bufs, gpost, py_pool, ytmp, ss2, rstd2, lhs_fn=None, lhs_bufs=None,
             use_ln=False):
        P = self.P
        junk = self.junk
        if lhs_fn is None:
            lhs_fn = lambda k, s: actT[:, k, s * 128:(s + 1) * 128]
            lhs_bufs = [actT]
        P.op("dve", lambda e: e.memset(ss2[:], 0.0), [], [ss2])
        for s in range(4):
            for n in range(2):
                py = py_pool.next()
                for k in range(nk):
                    P.op("pe", lambda e, s=s, n=n, k=k, py=py: e.matmul(
                        py[:], lhsT=lhs_fn(k, s), rhs=w_rhs(k, n),
                        start=(k == 0), stop=(k == nk - 1)), list(lhs_bufs) + list(w_bufs), [py])
                P.op("act", lambda e, s=s, n=n, py=py: e.activation(out=junk[:, 0:512], in_=py[:], func=AF.Square,
                                                                     accum_out=ss2[:, 2 * s + n:2 * s + n + 1]),
                     [py], [junk, ss2, py])
                P.op("dve", lambda e, n=n, py=py: e.tensor_copy(out=ytmp[:, n * 512:(n + 1) * 512], in_=py[:]),
                     [py], [ytmp])
            P.op("dve", lambda e, s=s: e.tensor_tensor(out=rstd2[:, s:s + 1], in0=ss2[:, 2 * s:2 * s + 1],
                                                       in1=ss2[:, 2 * s + 1:2 * s + 2], op=ALU.add), [ss2], [rstd2])
            if use_ln:
                self.rsqrt_lnexp(rstd2[:, s:s + 1], rstd2[:, s:s + 1], 1.0 / D, [rstd2], [rstd2])
            else:
                P.op("act", lambda e, s=s: e.activation(out=rstd2[:, s:s + 1], in_=rstd2[:, s:s + 1], func=AF.Sqrt,
                                                         scale=1.0 / D, bias=self.eps_t[:]), [rstd2, self.eps_t], [rstd2])
                P.op("dve", lambda e, s=s: e.reciprocal(out=rstd2[:, s:s + 1], in_=rstd2[:, s:s + 1]), [rstd2], [rstd2])
            P.op("dve", lambda e, s=s: e.scalar_tensor_tensor(out=ytmp[:], in0=ytmp[:], scalar=rstd2[:, s:s + 1],
                                                               in1=gpost[:], op0=ALU.mult, op1=ALU.mult),
                 [ytmp, rstd2, gpost], [ytmp])
            P.op("dve", lambda e, s=s: e.tensor_tensor(out=hT[:, s, :], in0=hT[:, s, :], in1=ytmp[:], op=ALU.add),
                 [hT, ytmp], [hT])
        tile = dst[t * TT:(t + 1) * TT, :].rearrange("(s p) d -> p s d", p=128)
        kd = self.k_out if dst is self.out else self.k_h
        P.dma(tile, hT[:], reads=[hT], writes=[kd])

    def ffn(self, l, f, src, dst):
        P = self.P
        m = P.mark()
        sub = 0 if f == 0 else 3
        gpre = P.sbuf("gpre", [128, D], F32)
        gpost = P.sbuf("gpost", [128, D], F32)
        self.load_gain(gpre, self.norm_pre[l, sub])
        self.load_gain(gpost, self.norm_post[l, sub], scale=0.5)
        wo = P.sbuf("wo", [128, 22, D], BF16)
        hT_p = P.pool("hT", [128, 4, D], F32, 2)
        hn = P.sbuf("hn", [128, 4, D], BF16)
        hnT_p = P.pool("hnT", [128, 8, TT], BF16, 2)
        actT = P.sbuf("actT", [128, 22, TT], BF16)
        sg_p = P.pool("sg", [128, TT], F32, 2)
        ytmp = P.sbuf("ytmp", [128, D], F32)
        ss = P.sbuf("ss", [128, 4], F32)
        rstd = P.sbuf("rstd", [128, 4], F32)
        ss2 = P.sbuf("ss2", [128, 8], F32)
        rstd2 = P.sbuf("rstd2", [128, 4], F32)
        slab_p = P.pool("slab", [128, 8, 2, 512], BF16, 3)
        tp_p = P.pool("tp", [128, 2 * TT], BF16, 2, psum=True)
        pg_p = P.pool("pg", [128, TT], F32, 2, psum=True)
        pu_p = P.pool("pu", [128, TT], F32, 2, psum=True)
        py_p = P.pool("py", [128, TT], F32, 2, psum=True)

        slabs = {}
        nload = [0]

        def ensure(i):
            while nload[0] <= i and nload[0] < NT * NSLAB:
                jg_ = nload[0] % NSLAB
                sl = slab_p.next()
                P.dma(sl[:], self.wi_s[l, f, jg_], reads=[self.kprep[(l, "ffn%d" % f)]], writes=[sl])
                slabs[nload[0]] = sl
                nload[0] += 1

        hTs = [None] * NT
        hnTs = [None] * NT
        hTs[0] = hT_p.next()
        hnTs[0] = hnT_p.next()
        self.front(src, 0, hTs[0], gpre, hn, hnTs[0], ss, rstd, tp_p)
        ensure(1)
        P.dma(wo[:], self.wo_s[l, f].rearrange("(k p) n -> p k n", p=128), reads=[self.kprep[(l, "ffn%d" % f)]], writes=[wo])
        for t in range(NT):
            hT, hnT = hTs[t], hnTs[t]
            for jg in range(NSLAB):
                ensure(t * NSLAB + jg + 2)
                sl = slabs.pop(t * NSLAB + jg)
                nj = 4 if jg < 5 else 2
                for jj in range(nj):
                    j = jg * 4 + jj
                    pg, pu = pg_p.next(), pu_p.next()
                    for k in range(8):
                        P.op("pe", lambda e, k=k, jj=jj, pg=pg, sl=sl, hnT=hnT: e.matmul(
                            pg[:], lhsT=sl[:, k, 0, jj * 128:(jj + 1) * 128], rhs=hnT[:, k, :],
                            start=(k == 0), stop=(k == 7)), [sl, hnT], [pg])
                    for k in range(8):
                        P.op("pe", lambda e, k=k, jj=jj, pu=pu, sl=sl, hnT=hnT: e.matmul(
                            pu[:], lhsT=sl[:, k, 1, jj * 128:(jj + 1) * 128], rhs=hnT[:, k, :],
                            start=(k == 0), stop=(k == 7)), [sl, hnT], [pu])
                    sg = sg_p.next()
                    P.op("act", lambda e, pg=pg, sg=sg: e.activation(out=sg[:], in_=pg[:], func=AF.Silu), [pg], [sg])
                    P.op("dve", lambda e, j=j, pu=pu, sg=sg: e.tensor_tensor(out=actT[:, j, :], in0=pu[:], in1=sg[:],
                                                                              op=ALU.mult), [pu, sg], [actT])
                if jg == 1 and t + 1 < NT:
                    hTs[t + 1] = hT_p.next()
                    hnTs[t + 1] = hnT_p.next()
                    self.front_a(src, t + 1, hTs[t + 1], gpre, hn, ss, rstd)
            if t + 1 < NT:
                self.front_b(hn, hnTs[t + 1], tp_p)
            self.back(dst, t, hT, actT, 22, lambda k, n: wo[:, k, n * 512:(n + 1) * 512], [wo], gpost,
                      py_p, ytmp, ss2, rstd2)
        P.barrier()
        P.release(m)

    def xattn(self, l, src, dst):
        P = self.P
        ident, junk, ones_m, negones = self.ident, self.junk, self.ones_m, self.negones
        kp = self.kprep[(l, "xa")]
        m0 = P.mark()
        gpre = P.sbuf("gpre", [128, D], F32)
        gpost = P.sbuf("gpost", [128, D], F32)
        self.load_gain(gpre, self.norm_pre[l, 2])
        self.load_gain(gpost, self.norm_post[l, 2])
        wq = P.sbuf("wq", [128, 8, D], BF16)
        wox = P.sbuf("wox", [128, 8, D], BF16)
        kT = P.sbuf("kT", [128, 8, MEM], BF16)
        v_tm = P.sbuf("v_tm", [128, 2, D], BF16)
        kmax2 = P.sbuf("kmax2", [128, 4], F32)
        tp_p = P.pool("tp", [128, 2 * TT], BF16, 2, psum=True)
        pa_p = P.pool("pa", [128, TT], F32, 3, psum=True)
        po_p = P.pool("po", [128, TT], F32, 2, psum=True)
        pd_p = P.pool("pd", [128, TT], F32, 1, psum=True)
        ss = P.sbuf("ss", [128, 4], F32)
        rstd = P.sbuf("rstd", [128, 4], F32)
        m1 = P.mark()
        wkv = P.sbuf("wkv", [128, 8, 2 * D], BF16)
        P.dma(wkv[:], self.wkv_s[l].rearrange("(k p) n -> p k n", p=128), reads=[kp], writes=[wkv])
        gmem = P.sbuf("gmem", [128, D], F32)
        self.load_gain(gmem, self.mem_norm_g[l])
        mT = P.sbuf("mT", [128, 2, D], F32)
        memn = P.sbuf("memn", [128, 2, D], BF16)
        memnT = P.sbuf("memnT", [128, 8, MEM], BF16)
        ksqT = P.sbuf("ksqT", [128, 8, MEM], BF16)
        P.dma(mT[:], self.mem.rearrange("(s p) d -> p s d", p=128), writes=[mT])
        P.dma(wq[:], self.wq_s[l].rearrange("(k p) n -> p k n", p=128), reads=[kp], writes=[wq])
        P.dma(wox[:], self.wox_s[l].rearrange("(k p) n -> p k n", p=128), reads=[kp], writes=[wox])
        for s_ in range(2):
            P.op("act", lambda e, s_=s_: e.activation(out=junk[:], in_=mT[:, s_, :], func=AF.Square,
                                                       accum_out=ss[:, s_:s_ + 1]), [mT], [junk, ss])
        self.rsqrt_lnexp(rstd[:, 0:2], ss[:, 0:2], 1.0 / D, [ss], [rstd])
        for s_ in range(2):
            P.op("dve", lambda e, s_=s_: e.scalar_tensor_tensor(out=memn[:, s_, :], in0=mT[:, s_, :],
                                                                 scalar=rstd[:, s_:s_ + 1], in1=gmem[:],
                                                                 op0=ALU.mult, op1=ALU.mult), [mT, rstd, gmem], [memn])
        for c4 in range(2):
            tp = tp_p.next()
            for cc in range(4):
                c = 4 * c4 + cc
                for s_ in range(2):
                    P.op("pe", lambda e, s_=s_, c=c, cc=cc, tp=tp: e.transpose(
                        tp[:, cc * 256 + s_ * 128:cc * 256 + (s_ + 1) * 128], memn[:, s_, c * 128:(c + 1) * 128], ident[:]),
                        [memn, ident], [tp])
            P.op("act", lambda e, c4=c4, tp=tp: e.activation(out=memnT[:, 4 * c4:4 * c4 + 4, :],
                                                            in_=tp[:].rearrange("p (c t) -> p c t", c=4), func=AF.Copy),
                 [tp], [memnT])
        for c in range(8):
            pa = pa_p.next()
            for k in range(8):
                P.op("pe", lambda e, k=k, c=c, pa=pa: e.matmul(pa[:, 0:MEM], lhsT=wkv[:, k, c * 128:(c + 1) * 128],
                                                                rhs=memnT[:, k, :], start=(k == 0), stop=(k == 7)),
                     [wkv, memnT], [pa])
            P.op("act", lambda e, c=c, pa=pa: e.activation(out=ksqT[:, c, :], in_=pa[:, 0:MEM], func=AF.Square),
                 [pa], [ksqT, pa])
            P.op("dve", lambda e, c=c, pa=pa: e.tensor_copy(out=kT[:, c, :], in_=pa[:, 0:MEM]), [pa], [kT])
        for mb in range(2):
            for n in range(2):
                pa = pa_p.next()
                for k in range(8):
                    P.op("pe", lambda e, k=k, mb=mb, n=n, pa=pa: e.matmul(
                        pa[:], lhsT=memnT[:, k, mb * 128:(mb + 1) * 128], rhs=wkv[:, k, D + n * 512:D + (n + 1) * 512],
                        start=(k == 0), stop=(k == 7)), [wkv, memnT], [pa])
                P.op("act", lambda e, mb=mb, n=n, pa=pa: e.activation(out=v_tm[:, mb, n * 512:(n + 1) * 512], in_=pa[:],
                                                                       func=AF.Copy), [pa], [v_tm])
        for h in range(4):
            pd = pd_p.next()
            for cc in range(2):
                P.op("pe", lambda e, h=h, cc=cc, pd=pd: e.matmul(pd[:, 0:MEM], lhsT=ones_m[:], rhs=ksqT[:, 2 * h + cc, :],
                                                                  start=(cc == 0), stop=(cc == 1)), [ones_m, ksqT], [pd])
            P.op("dve", lambda e, h=h, pd=pd: e.reduce_max(out=kmax2[:, h:h + 1], in_=pd[:, 0:MEM],
                                                           axis=mybir.AxisListType.X), [pd], [kmax2])
        P.op("dve", lambda e: e.tensor_scalar(out=kmax2[:], in0=kmax2[:], scalar1=1.05 / 256.0, scalar2=None,
                                              op0=ALU.mult), [kmax2], [kmax2])
        P.barrier()
        P.release(m1)
        hT_p = P.pool("hT", [128, 4, D], F32, 2)
        hn = P.sbuf("hn", [128, 4, D], BF16)
        hnT_p = P.pool("hnT", [128, 8, TT], BF16, 2)
        qT = P.sbuf("qT", [128, 8, TT], BF16)
        qsqT = P.sbuf("qsqT", [128, 8, TT], BF16)
        mrow = P.sbuf("mrow", [128, 4, TT], BF16)
        lnt = P.sbuf("lnt", [128, TT], F32)
        pT_p = P.pool("pT", [128, 2, TT], BF16, 2)
        rden_p = P.pool("rden", [128, TT], F32, 2)
        oT = P.sbuf("oT", [128, 8, TT], BF16)
        ytmp = P.sbuf("ytmp", [128, D], F32)
        ss2 = P.sbuf("ss2", [128, 8], F32)
        rstd2 = P.sbuf("rstd2", [128, 4], F32)
        hTs = [None] * NT
        hnTs = [None] * NT
        hTs[0] = hT_p.next()
        hnTs[0] = hnT_p.next()
        self.front(src, 0, hTs[0], gpre, hn, hnTs[0], ss, rstd, tp_p, use_ln=True)
        for t in range(NT):
            hT, hnT = hTs[t], hnTs[t]
            for c in range(8):
                pa = pa_p.next()
                for k in range(8):
                    P.op("pe", lambda e, k=k, c=c, pa=pa, hnT=hnT: e.matmul(
                        pa[:], lhsT=wq[:, k, c * 128:(c + 1) * 128], rhs=hnT[:, k, :], start=(k == 0), stop=(k == 7)),
                        [wq, hnT], [pa])
                P.op("act", lambda e, c=c, pa=pa: e.activation(out=qsqT[:, c, :], in_=pa[:], func=AF.Square),
                     [pa], [qsqT, pa])
                P.op("dve", lambda e, c=c, pa=pa: e.tensor_scalar(out=qT[:, c, :], in0=pa[:], scalar1=1.0 / 16.0,
                                                                   scalar2=None, op0=ALU.mult), [pa], [qT])
            for h in range(4):
                pd = pd_p.next()
                for cc in range(2):
                    P.op("pe", lambda e, h=h, cc=cc, pd=pd: e.matmul(pd[:], lhsT=ones_m[:], rhs=qsqT[:, 2 * h + cc, :],
                                                                      start=(cc == 0), stop=(cc == 1)), [ones_m, qsqT], [pd])
                P.op("act", lambda e, h=h, pd=pd: e.activation(out=lnt[:], in_=pd[:], func=AF.Ln,
                                                               scale=kmax2[:, h:h + 1], bias=self.eps_t[:]),
                     [pd, kmax2, self.eps_t], [lnt])
                P.op("act", lambda e, h=h: e.activation(out=mrow[:, h, :], in_=lnt[:], func=AF.Exp, scale=0.5), [lnt], [mrow])
            def xa_scores(h):
                pT = pT_p.next()
                for mb in range(2):
                    pa = pa_p.next()
                    for cc in range(2):
                        P.op("pe", lambda e, h=h, mb=mb, cc=cc, pa=pa: e.matmul(
                            pa[:], lhsT=kT[:, 2 * h + cc, mb * 128:(mb + 1) * 128], rhs=qT[:, 2 * h + cc, :],
                            start=(cc == 0), stop=False), [kT, qT], [pa])
                    P.op("pe", lambda e, h=h, pa=pa: e.matmul(pa[:], lhsT=self.neg128[:], rhs=mrow[:, h, :],
                                                              start=False, stop=True), [self.neg128, mrow], [pa])
                    P.op("act", lambda e, mb=mb, pa=pa, pT=pT: e.activation(out=pT[:, mb, :], in_=pa[:], func=AF.Exp),
                         [pa], [pT])
                return pT

            def xa_pv(h, pT):
                pd = pd_p.next()
                for mb in range(2):
                    P.op("pe", lambda e, mb=mb, pd=pd, pT=pT: e.matmul(pd[:], lhsT=ones_m[:], rhs=pT[:, mb, :],
                                                                        start=(mb == 0), stop=(mb == 1)), [ones_m, pT], [pd])
                rden = rden_p.next()
                P.op("act", lambda e, pd=pd, rden=rden: e.activation(out=rden[:], in_=pd[:], func=AF.Ln), [pd], [rden])
                P.op("act", lambda e, rden=rden: e.activation(out=rden[:], in_=rden[:], func=AF.Exp, scale=-1.0), [rden], [rden])
                for ec in range(2):
                    po = po_p.next()
                    for mb in range(2):
                        P.op("pe", lambda e, h=h, ec=ec, mb=mb, po=po, pT=pT: e.matmul(
                            po[:], lhsT=v_tm[:, mb, h * 256 + ec * 128:h * 256 + (ec + 1) * 128], rhs=pT[:, mb, :],
                            start=(mb == 0), stop=(mb == 1)), [v_tm, pT], [po])
                    P.op("dve", lambda e, h=h, ec=ec, po=po, rden=rden: e.tensor_tensor(
                        out=oT[:, 2 * h + ec, :], in0=po[:], in1=rden[:], op=ALU.mult), [po, rden], [oT])

            pTs = xa_scores(0)
            for h in range(4):
                nxt_pT = xa_scores(h + 1) if h + 1 < 4 else None
                if h == 1 and t + 1 < NT:
                    hTs[t + 1] = hT_p.next()
                    hnTs[t + 1] = hnT_p.next()
                    self.front_a(src, t + 1, hTs[t + 1], gpre, hn, ss, rstd, use_ln=True)
                xa_pv(h, pTs)
                pTs = nxt_pT
            if t + 1 < NT:
                self.front_b(hn, hnTs[t + 1], tp_p)
            self.back(dst, t, hT, oT, 8, lambda k, n: wox[:, k, n * 512:(n + 1) * 512], [wox], gpost,
                      pa_p, ytmp, ss2, rstd2, use_ln=True)
        P.barrier()
        P.release(m0)

    def mixer(self, l, src, dst):
        self.mixer_a(l, src)
        self.mixer_b(l)
        self.mixer_c(l, src, dst)

    def mixer_a(self, l, src):
        P = self.P
        ident, junk, ones_m = self.ident, self.junk, self.ones_m
        eps_t = self.eps_t
        kp = self.kprep[(l, "mix")]
        m0 = P.mark()
        w_in = P.sbuf("w_in", [128, 8, NIN], BF16)
        P.dma(w_in[:], self.win_s[l].rearrange("(k p) n -> p k n", p=128), reads=[kp], writes=[w_in])
        gpre = P.sbuf("gpre", [128, D], F32)
        self.load_gain(gpre, self.norm_pre[l, 1])
        lng = P.sbuf("lng", [128, 256], F32)
        P.dma(lng[:], self.sgu_ln_g[l].partition_broadcast(128), writes=[lng])
        Bb = P.sbuf("Bb", [128, 512], F32)
        P.dma(Bb[:], self.sgu_b[l].rearrange("g t -> (g t)").partition_broadcast(128), writes=[Bb])
        ng = P.sbuf("ng", [128, 512], F32)
        P.dma(ng[:], self.ssm_norm_g[l].partition_broadcast(128), writes=[ng])
        dtb = P.sbuf("dtb", [128, 8], F32)
        P.dma(dtb[:], self.dt_bias[l].partition_broadcast(128), writes=[dtb])
        Aneg = P.sbuf("Aneg", [128, 8], F32)
        P.dma(Aneg[:], self.a_log[l].partition_broadcast(128), writes=[Aneg])
        P.op("act", lambda e: e.activation(out=Aneg[:], in_=Aneg[:], func=AF.Exp), [Aneg], [Aneg])
        P.op("dve", lambda e: e.tensor_scalar(out=Aneg[:], in0=Aneg[:], scalar1=-1.0, scalar2=None, op0=ALU.mult),
             [Aneg], [Aneg])
        Dd = P.sbuf("Dd", [128, 8], F32)
        P.dma(Dd[:], self.ssm_d[l].partition_broadcast(128), writes=[Dd])
        cw = P.sbuf("cw", [128, 4, 8], F32)
        for k in range(4):
            P.dma(cw[:, k, :], self.conv_w[l, k].rearrange("(c p) -> p c", p=128), writes=[cw],
                  allow_slow_non_contiguous=True)
        cb = P.sbuf("cb", [128, 8], F32)
        P.dma(cb[:], self.conv_b[l].rearrange("(c p) -> p c", p=128), writes=[cb], allow_slow_non_contiguous=True)
        U = P.sbuf("U", [128, 128], F32)
        Lm = P.sbuf("Lm", [128, 128], F32)
        ones_f = P.sbuf("ones_f", [128, 128], F32)
        P.dma(U[:], self.c_tri[0], writes=[U])
        P.dma(Lm[:], self.c_tri[1], writes=[Lm])
        P.dma(ones_f[:], self.c_tri[2], writes=[ones_f])
        pf = P.pool("pf", [128, TT], F32, 5, psum=True)
        pyo_p = P.pool("pyo", [128, TT], F32, 1, psum=True)
        pb = P.pool("pb", [128, 2 * TT], BF16, 2, psum=True)
        WsT = P.sbuf("WsT", [128, 4, 128], BF16)
        m1 = P.mark()
        ws_nat = P.sbuf("ws_nat", [128, 4, 128], F32)
        ws_bf = P.sbuf("ws_bf", [128, 4, 128], BF16)
        P.dma(ws_nat[:], self.sgu_w[l].rearrange("g t s -> t g s"), writes=[ws_nat])
        P.op("dve", lambda e: e.tensor_copy(out=ws_bf[:], in_=ws_nat[:]), [ws_nat], [ws_bf])
        tp = pb.next()
        for g in range(4):
            P.op("pe", lambda e, g=g, tp=tp: e.transpose(tp[:, g * 128:(g + 1) * 128], ws_bf[:, g, :], ident[:]),
                 [ws_bf, ident], [tp])
        P.op("dve", lambda e, tp=tp: e.tensor_tensor(out=WsT[:], in0=tp[:, 0:512].rearrange("p (g t) -> p g t", g=4),
                                                     in1=U[:].unsqueeze(1).to_broadcast([128, 4, 128]), op=ALU.mult),
             [tp, U], [WsT])
        P.barrier()
        P.release(m1)
        hT = P.sbuf("hT", [128, 4, D], F32)
        hn = P.sbuf("hn", [128, 4, D], BF16)
        hnT = P.sbuf("hnT", [128, 8, TT], BF16)
        ss = P.sbuf("ss", [128, 4], F32)
        rstd = P.sbuf("rstd", [128, 4], F32)
        uT = P.sbuf("uT", [64, 4, TT], BF16)
        vg = P.sbuf("vg", [128, 4, 256], F32)
        vtmp = P.sbuf("vtmp", [128, 256], F32)
        vn = P.sbuf("vn", [128, 4, 256], BF16)
        vst = P.sbuf("vst", [128, 24], F32)
        zs = P.sbuf("zs", [128, 4, TT], F32)
        xbcT = P.sbuf("xbcT", [128, 8, TT + 4], BF16)
        xact = P.sbuf("xact", [128, 8, TT], BF16)
        dg = P.sbuf("dg", [128, 32, 128], BF16)
        for c in range(8):
            for k in range(4):
                P.op("dve", lambda e, c=c, k=k: e.tensor_scalar(out=dg[:, c * 4 + k, :], in0=self.ident_f[:],
                                                                 scalar1=cw[:, k, c:c + 1], scalar2=None, op0=ALU.mult),
                     [self.ident_f, cw], [dg])
        dts = P.sbuf("dts", [128, 6, 32], F32)
        a_t = P.sbuf("a_t", [128, 4, 8], F32)
        qkvst = P.sbuf("qkvst", [128, 6, TT], BF16)
        aoutT = P.sbuf("aoutT", [64, 4, TT], BF16)
        boutT = P.sbuf("boutT", [128, 4, TT], BF16)
        sgt = P.sbuf("sgt", [64, 512], F32)
        B_tm = P.sbuf("B_tm", [128, 256], BF16)
        x_tm = P.sbuf("x_tm", [128, 512], BF16)
        xsD_p = P.pool("xsD", [128, 512], F32, 1)
        ypre_p = P.pool("ypre", [128, 512], F32, 2)
        stS_p = P.pool("stS", [128, 512], F32, 2)
        cbm = P.sbuf("cbm", [128, 2, 128], F32)
        R = P.sbuf("R", [128, 8, 128], F32)
        decT = P.sbuf("decT", [128, 8, 128], F32)
        eacs_p = P.pool("eacs", [128, 16], F32, 2)
        MT = P.sbuf("MT", [128, 8, 128], BF16)
        xd = P.sbuf("xd", [128, 512], BF16)
        y = P.sbuf("y", [128, 512], F32)
        ynb = P.sbuf("ynb", [128, 512], BF16)
        hst = P.sbuf("hst", [128, 512], F32)
        hst_bf = P.sbuf("hst_bf", [128, 512], BF16)
        ssg = P.sbuf("ssg", [128, 4], F32)
        P.op("dve", lambda e: e.memset(hst[:], 0.0), [], [hst])
        P.op("dve", lambda e: e.memset(hst_bf[:], 0.0), [], [hst_bf])
        P.op("dve", lambda e: e.memset(xbcT[:, :, 0:3], 0.0), [], [xbcT])

        def proj_fm(pbuf, M, col0):
            for k in range(8):
                P.op("pe", lambda e, k=k: e.matmul(pbuf[0:M, :], lhsT=w_in[:, k, col0:col0 + M], rhs=hnT[:, k, :],
                                                   start=(k == 0), stop=(k == 7)), [w_in, hnT], [pbuf])

        def proj_tm(pview, pbuf, s_, col0, N):
            for k in range(8):
                P.op("pe", lambda e, k=k: e.matmul(pview, lhsT=hnT[:, k, s_ * 128:(s_ + 1) * 128],
                                                   rhs=w_in[:, k, col0:col0 + N], start=(k == 0), stop=(k == 7)),
                     [w_in, hnT], [pbuf])

        for t in range(NT):
            self.front(src, t, hT, gpre, hn, hnT, ss, rstd, pb, use_ln=True)
            for g in range(4):
                pu = pf.next()
                proj_fm(pu, 64, g * 64)
                P.op("act", lambda e, g=g, pu=pu: e.activation(out=uT[0:64, g, :], in_=pu[0:64, :],
                                                               func=AF.Gelu_apprx_tanh), [pu], [uT])
            for s2 in range(2):
                pv = pf.next()
                for s1 in range(2):
                    s_ = 2 * s2 + s1
                    proj_tm(pv[:, s1 * 256:(s1 + 1) * 256], pv, s_, 256, 256)
                for s1 in range(2):
                    s_ = 2 * s2 + s1
                    P.op("act", lambda e, s_=s_, s1=s1, pv=pv: e.activation(
                        out=vg[:, s_, :], in_=pv[:, s1 * 256:(s1 + 1) * 256], func=AF.Gelu_apprx_tanh,
                        accum_out=vst[:, s_:s_ + 1]), [pv], [vg, vst])
                    P.op("act", lambda e, s_=s_: e.activation(out=junk[:, 0:256], in_=vg[:, s_, :], func=AF.Square,
                                                              accum_out=vst[:, 4 + s_:5 + s_]), [vg], [junk, vst])
            P.op("dve", lambda e: e.tensor_scalar(out=vst[:, 8:12], in0=vst[:, 0:4], scalar1=1.0 / 256, scalar2=None,
                                                  op0=ALU.mult), [vst], [vst])
            P.op("dve", lambda e: e.tensor_tensor(out=vst[:, 12:16], in0=vst[:, 8:12], in1=vst[:, 8:12], op=ALU.mult),
                 [vst], [vst])
            P.op("dve", lambda e: e.scalar_tensor_tensor(out=vst[:, 16:20], in0=vst[:, 4:8], scalar=1.0 / 256,
                                                         in1=vst[:, 12:16], op0=ALU.mult, op1=ALU.subtract), [vst], [vst])
            self.rsqrt_lnexp(vst[:, 20:24], vst[:, 16:20], 1.0, [vst], [vst])
            for s_ in range(4):
                P.op("dve", lambda e, s_=s_: e.tensor_scalar(out=vtmp[:], in0=vg[:, s_, :], scalar1=vst[:, 8 + s_:9 + s_],
                                                             scalar2=vst[:, 20 + s_:21 + s_], op0=ALU.subtract,
                                                             op1=ALU.mult), [vg, vst], [vtmp])
                P.op("dve", lambda e, s_=s_: e.tensor_tensor(out=vn[:, s_, :], in0=vtmp[:], in1=lng[:], op=ALU.mult),
                     [vtmp, lng], [vn])
            for s_ in range(4):
                pz = pf.next()
                proj_tm(pz[:], pz, s_, 512, 512)
                P.op("act", lambda e, s_=s_, pz=pz: e.activation(out=zs[:, s_, :], in_=pz[:], func=AF.Silu), [pz], [zs])
            for c in range(8):
                px = pf.next()
                proj_fm(px, 128, 1024 + c * 128)
                if c % 2 == 0:
                    P.op("act", lambda e, c=c, px=px: e.activation(out=xbcT[:, c, 3:3 + TT], in_=px[:], func=AF.Copy),
                         [px], [xbcT])
                else:
                    P.op("dve", lambda e, c=c, px=px: e.tensor_copy(out=xbcT[:, c, 3:3 + TT], in_=px[:]), [px], [xbcT])
            for c in range(8):
                pc = pf.next()
                for k in range(4):
                    P.op("pe", lambda e, c=c, k=k, pc=pc: e.matmul(pc[:], lhsT=dg[:, c * 4 + k, :], rhs=xbcT[:, c, k:k + TT],
                                                                    start=(k == 0), stop=(k == 3)), [dg, xbcT], [pc])
                P.op("act", lambda e, c=c, pc=pc: e.activation(out=xact[:, c, :], in_=pc[:], func=AF.Silu,
                                                               bias=cb[:, c:c + 1]), [pc, cb], [xact])
            P.op("dve", lambda e: e.tensor_copy(out=xbcT[:, :, 0:3], in_=xbcT[:, :, TT:TT + 3]), [xbcT], [xbcT])
            pdt = pf.next()
            for s_ in range(4):
                proj_tm(pdt[:, s_ * 8:(s_ + 1) * 8], pdt, s_, 2048, 8)
            P.op("dve", lambda e, pdt=pdt: e.tensor_tensor(out=dts[:, 0, :].rearrange("p (s h) -> p s h", s=4),
                                                           in0=pdt[:, 0:32].rearrange("p (s h) -> p s h", s=4),
                                                           in1=dtb[:].unsqueeze(1).to_broadcast([128, 4, 8]), op=ALU.add),
                 [pdt, dtb], [dts])
            P.op("act", lambda e: e.activation(out=dts[:, 1, :], in_=dts[:, 0, :], func=AF.Exp), [dts], [dts])
            P.op("act", lambda e: e.activation(out=dts[:, 2, :], in_=dts[:, 1, :], func=AF.Ln, bias=1.0), [dts], [dts])
            P.op("dve", lambda e: e.tensor_scalar(out=dts[:, 3, :], in0=dts[:, 1, :], scalar1=1.0 / 3.0, scalar2=-0.5,
                                                  op0=ALU.mult, op1=ALU.add), [dts], [dts])
            P.op("dve", lambda e: e.tensor_tensor(out=dts[:, 3, :], in0=dts[:, 3, :], in1=dts[:, 1, :], op=ALU.mult),
                 [dts], [dts])
            P.op("dve", lambda e: e.scalar_tensor_tensor(out=dts[:, 3, :], in0=dts[:, 3, :], scalar=1.0, in1=dts[:, 1, :],
                                                         op0=ALU.add, op1=ALU.mult), [dts], [dts])
            P.op("dve", lambda e: e.tensor_single_scalar(out=dts[:, 4, :], in_=dts[:, 1, :], scalar=0.01, op=ALU.is_lt),
                 [dts], [dts])
            P.op("dve", lambda e: e.tensor_tensor(out=dts[:, 3, :], in0=dts[:, 3, :], in1=dts[:, 2, :], op=ALU.subtract),
                 [dts], [dts])
            P.op("dve", lambda e: e.tensor_tensor(out=dts[:, 3, :], in0=dts[:, 3, :], in1=dts[:, 4, :], op=ALU.mult),
                 [dts], [dts])
            P.op("dve", lambda e: e.tensor_tensor(out=dts[:, 5, :], in0=dts[:, 3, :], in1=dts[:, 2, :], op=ALU.add),
                 [dts], [dts])
            P.op("dve", lambda e: e.tensor_tensor(out=a_t[:], in0=dts[:, 5, :].rearrange("p (s h) -> p s h", s=4),
                                                  in1=Aneg[:].unsqueeze(1).to_broadcast([128, 4, 8]), op=ALU.mult),
                 [dts, Aneg], [a_t])
            def qkv_proj(j):
                pq = pf.next()
                proj_fm(pq, 128, 2056 + j * 128)
                P.op("act", lambda e: e.activation(out=qkvst[:, j, :], in_=pq[:], func=AF.Copy,
                                                   scale=(0.125 if j < 2 else 1.0)), [pq], [qkvst])

            def stage_a(s_):
                l0 = s_ * 128
                xsD, eacs = xsD_p.next(), eacs_p.next()
                psg = pf.next()
                for g in range(4):
                    P.op("pe", lambda e, g=g: e.matmul(
                        psg[0:64, g * 128:(g + 1) * 128], lhsT=vn[:, s_, g * 64:(g + 1) * 64], rhs=WsT[:, g, :],
                        start=True, stop=True), [vn, WsT], [psg])
                P.op("dve", lambda e: e.tensor_tensor(out=sgt[0:64, :], in0=psg[0:64, :], in1=Bb[0:64, :], op=ALU.add),
                     [psg, Bb], [sgt])
                P.op("dve", lambda e: e.tensor_tensor(
                    out=aoutT[0:64, :, l0:l0 + 128], in0=sgt[0:64, :].rearrange("p (g t) -> p g t", g=4),
                    in1=uT[0:64, :, l0:l0 + 128], op=ALU.mult), [sgt, uT], [aoutT])
                tpx = pb.next()
                for c in range(4):
                    P.op("pe", lambda e, c=c: e.transpose(tpx[:, c * 128:(c + 1) * 128], xact[:, c, l0:l0 + 128], ident[:]),
                         [xact, ident], [tpx])
                for g in range(2):
                    P.op("pe", lambda e, g=g: e.transpose(tpx[:, 512 + g * 128:512 + (g + 1) * 128],
                                                          xact[:, 4 + g, l0:l0 + 128], ident[:]), [xact, ident], [tpx])
                P.op("act", lambda e: e.activation(out=B_tm[:], in_=tpx[:, 512:768], func=AF.Copy), [tpx], [B_tm, tpx])
                P.op("dve", lambda e: e.tensor_tensor(
                    out=x_tm[:].rearrange("p (h q) -> p h q", h=8), in0=tpx[:, 0:512].rearrange("p (h q) -> p h q", h=8),
                    in1=dts[:, 5, s_ * 8:(s_ + 1) * 8].unsqueeze(2).to_broadcast([128, 8, 64]), op=ALU.mult),
                    [tpx, dts], [x_tm])
                P.op("dve", lambda e: e.tensor_tensor(
                    out=xsD[:].rearrange("p (h q) -> p h q", h=8), in0=tpx[:, 0:512].rearrange("p (h q) -> p h q", h=8),
                    in1=Dd[:].unsqueeze(2).to_broadcast([128, 8, 64]), op=ALU.mult), [tpx, Dd], [xsD])
                pcb = pf.next()
                for g in range(2):
                    P.op("pe", lambda e, g=g: e.matmul(
                        pcb[:, g * 128:(g + 1) * 128], lhsT=xact[:, 4 + g, l0:l0 + 128], rhs=xact[:, 6 + g, l0:l0 + 128],
                        start=True, stop=True), [xact], [pcb])
                P.op("dve", lambda e: e.tensor_tensor(
                    out=cbm[:], in0=pcb[:, 0:256].rearrange("p (g l) -> p g l", g=2),
                    in1=U[:].unsqueeze(1).to_broadcast([128, 2, 128]), op=ALU.mult), [pcb, U], [cbm])
                P.op("dve", lambda e: e.tensor_tensor(
                    out=R[:], in0=a_t[:, s_, :].unsqueeze(2).to_broadcast([128, 8, 128]),
                    in1=U[:].unsqueeze(1).to_broadcast([128, 8, 128]), op=ALU.mult), [a_t, U], [R])
                psg2 = [pf.next(), pf.next()]
                for b_ in range(2):
                    P.op("pe", lambda e, b_=b_: e.matmul(
                        psg2[b_][:], lhsT=Lm[:], rhs=R[:, 4 * b_:4 * b_ + 4, :].rearrange("p h l -> p (h l)"),
                        start=True, stop=True), [Lm, R], [psg2[b_]])
                pac = pf.next()
                P.op("pe", lambda e: e.matmul(pac[:, 0:8], lhsT=U[:], rhs=a_t[:, s_, :], start=True, stop=True),
                     [U, a_t], [pac])
                P.op("pe", lambda e: e.matmul(pac[:, 8:16], lhsT=ones_f[:], rhs=a_t[:, s_, :], start=True, stop=True),
                     [ones_f, a_t], [pac])
                for b_ in range(2):
                    P.op("act", lambda e, b_=b_: e.activation(
                        out=decT[:, 4 * b_:4 * b_ + 4, :].rearrange("p h l -> p (h l)"), in_=psg2[b_][:], func=AF.Exp),
                        [psg2[b_]], [decT])
                P.op("act", lambda e: e.activation(out=eacs[:], in_=pac[:, 0:16], func=AF.Exp), [pac], [eacs])
                return dict(l0=l0, s_=s_, eacs=eacs, xsD=xsD)

            def stage_a2(st):
                l0, s_, eacs, xsD = st["l0"], st["s_"], st["eacs"], st["xsD"]
                P.op("dve", lambda e: e.tensor_tensor(
                    out=MT[:].rearrange("p (g e) l -> p g e l", g=2), in0=decT[:].rearrange("p (g e) l -> p g e l", g=2),
                    in1=cbm[:].unsqueeze(2).to_broadcast([128, 2, 4, 128]), op=ALU.mult), [decT, cbm], [MT])
                P.op("dve", lambda e: e.tensor_tensor(
                    out=xd[:].rearrange("p (h q) -> p h q", h=8), in0=x_tm[:].rearrange("p (h q) -> p h q", h=8),
                    in1=decT[:, :, 127:128].to_broadcast([128, 8, 64]), op=ALU.mult), [x_tm, decT], [xd])
                pyd = pf.next()
                for h in range(8):
                    P.op("pe", lambda e, h=h: e.matmul(pyd[:, h * 64:(h + 1) * 64], lhsT=MT[:, h, :],
                                                       rhs=x_tm[:, h * 64:(h + 1) * 64], start=True, stop=True),
                         [MT, x_tm], [pyd])
                pst = pf.next()
                for g in range(2):
                    P.op("pe", lambda e, g=g: e.matmul(
                        pst[:, g * 256:(g + 1) * 256], lhsT=B_tm[:, g * 128:(g + 1) * 128], rhs=xd[:, g * 256:(g + 1) * 256],
                        start=True, stop=True), [B_tm, xd], [pst])
                ypre, stS = ypre_p.next(), stS_p.next()
                P.op("dve", lambda e: e.tensor_tensor(out=ypre[:], in0=pyd[:], in1=xsD[:], op=ALU.add), [pyd, xsD], [ypre])
                P.op("act", lambda e: e.activation(out=stS[:], in_=pst[:], func=AF.Copy), [pst], [stS])
                st["ypre"] = ypre
                st["stS"] = stS
                return st

            def stage_b0(st):
                l0 = st["l0"]
                pyo = pyo_p.next()
                for g in range(2):
                    P.op("pe", lambda e, g=g: e.matmul(
                        pyo[:, g * 256:(g + 1) * 256], lhsT=xact[:, 6 + g, l0:l0 + 128], rhs=hst_bf[:, g * 256:(g + 1) * 256],
                        start=True, stop=True), [xact, hst_bf], [pyo])
                st["pyo"] = pyo

            def stage_b(st):
                l0, s_, eacs, ypre, stS, pyo = st["l0"], st["s_"], st["eacs"], st["ypre"], st["stS"], st["pyo"]
                P.op("dve", lambda e: e.tensor_tensor(
                    out=y[:].rearrange("p (h q) -> p h q", h=8), in0=pyo[:].rearrange("p (h q) -> p h q", h=8),
                    in1=eacs[:, 0:8].unsqueeze(2).to_broadcast([128, 8, 64]), op=ALU.mult), [pyo, eacs], [y])
                P.op("dve", lambda e: e.tensor_tensor(out=y[:], in0=y[:], in1=ypre[:], op=ALU.add), [y, ypre], [y])
                P.op("dve", lambda e: e.tensor_tensor(
                    out=hst[:].rearrange("p (h q) -> p h q", h=8), in0=hst[:].rearrange("p (h q) -> p h q", h=8),
                    in1=eacs[:, 8:16].unsqueeze(2).to_broadcast([128, 8, 64]), op=ALU.mult), [hst, eacs], [hst])
                P.op("dve", lambda e: e.tensor_tensor(out=hst[:], in0=hst[:], in1=stS[:], op=ALU.add), [hst, stS], [hst])
                P.op("act", lambda e: e.activation(out=hst_bf[:], in_=hst[:], func=AF.Copy), [hst], [hst_bf])
                P.op("dve", lambda e: e.tensor_tensor(out=y[:], in0=y[:], in1=zs[:, s_, :], op=ALU.mult), [y, zs], [y])
                for g in range(2):
                    P.op("act", lambda e, g=g: e.activation(out=junk[:, 0:256], in_=y[:, g * 256:(g + 1) * 256],
                                                            func=AF.Square, accum_out=ssg[:, g:g + 1]), [y], [junk, ssg])
                self.rsqrt_lnexp(ssg[:, 2:4], ssg[:, 0:2], 1.0 / 256, [ssg], [ssg])
                P.op("dve", lambda e: e.tensor_tensor(
                    out=y[:].rearrange("p (g q) -> p g q", g=2), in0=y[:].rearrange("p (g q) -> p g q", g=2),
                    in1=ssg[:, 2:4].unsqueeze(2).to_broadcast([128, 2, 256]), op=ALU.mult), [y, ssg], [y])
                P.op("dve", lambda e: e.tensor_tensor(out=ynb[:], in0=y[:], in1=ng[:], op=ALU.mult), [y, ng], [ynb])
                tpy = pb.next()
                for c in range(4):
                    P.op("pe", lambda e, c=c: e.transpose(tpy[:, c * 128:(c + 1) * 128], ynb[:, c * 128:(c + 1) * 128], ident[:]),
                         [ynb, ident], [tpy])
                P.op("act", lambda e: e.activation(
                    out=boutT[:, :, l0:l0 + 128], in_=tpy[:, 0:512].rearrange("p (c t) -> p c t", c=4), func=AF.Copy),
                    [tpy], [boutT])

            st_cur = stage_a2(stage_a(0))
            for s_ in range(4):
                stage_b0(st_cur)
                st_nxt = stage_a(s_ + 1) if s_ + 1 < 4 else None
                stage_b(st_cur)
                if st_nxt is not None:
                    st_nxt = stage_a2(st_nxt)
                qkv_proj(s_)
                if s_ + 4 < 6:
                    qkv_proj(s_ + 4)
                st_cur = st_nxt
            P.dma(self.qkv_s[:, :, t * TT:(t + 1) * TT].rearrange("j p t -> p j t"), qkvst[:], reads=[qkvst],
                  writes=[self.k_qkv])
            P.dma(self.a_s[:, :, t * TT:(t + 1) * TT].rearrange("g p t -> p g t"), aoutT[0:64, :, :], reads=[aoutT],
                  writes=[self.k_ab])
            P.dma(self.b_s[:, :, t * TT:(t + 1) * TT].rearrange("g p t -> p g t"), boutT[:], reads=[boutT],
                  writes=[self.k_ab])
        P.barrier()
        P.release(m0)

    def mixer_b(self, l):
        P = self.P
        ident, ident_f, junk, ones_m, negones, biasT = self.ident, self.ident_f, self.junk, self.ones_m, self.negones, self.biasT
        m0 = P.mark()
        qT = P.sbuf("qT", [128, 2, S], BF16)
        kT = P.sbuf("kT", [128, 2, S], BF16)
        vT = P.sbuf("vT", [128, 2, S], BF16)
        P.dma(qT[:], self.qkv_s[0:2].rearrange("j p t -> p j t"), reads=[self.k_qkv], writes=[qT])
        P.dma(kT[:], self.qkv_s[2:4].rearrange("j p t -> p j t"), reads=[self.k_qkv], writes=[kT])
        P.dma(vT[:], self.qkv_s[4:6].rearrange("j p t -> p j t"), reads=[self.k_qkv], writes=[vT])
        acc = P.sbuf("acc", [128, S], F32)
        cst = P.sbuf("cst", [64, S], BF16)
        rden_p = P.pool("rden", [64, TT], F32, 2)
        sq = P.sbuf("sq", [128, S], BF16)
        kh = P.sbuf("kh", [128, S], BF16)
        qh = P.sbuf("qh", [128, S], BF16)
        kmx = P.sbuf("kmx", [128, 16], F32)
        lnr = P.sbuf("lnr", [128, TT], F32)
        v_tm = [P.sbuf("v_tm%d" % p, [128, 32, 2, 128], BF16) for p in range(3)]
        for p in range(3):
            P.op("dve", lambda e, p=p: e.memset(v_tm[p][:, :, :, 64:128], 1.0), [], [v_tm[p]])
        pT_p = P.pool("pT", [128, 256], BF16, 3)
        ps_p = P.pool("ps", [128, TT], F32, 3, psum=True)
        pq_p = P.pool("pq", [128, TT], F32, 3, psum=True)
        pb = P.pool("pb", [128, 2 * TT], BF16, 2, psum=True)
        dils = (1, 4, 16)
        for c in range(2):
            for p, dil in enumerate(dils):
                nb = 32 // dil
                vv = vT[:, c, :].rearrange("p (l r) -> p r l", r=dil)
                for b8 in range(4):
                    tp = pb.next()
                    for bb in range(8):
                        b = b8 * 8 + bb
                        r, n = b // nb, b % nb
                        P.op("pe", lambda e, bb=bb, r=r, n=n, vv=vv, tp=tp: e.transpose(
                            tp[:, bb * 128:(bb + 1) * 128], vv[:, r, n * 128:(n + 1) * 128], ident[:]), [vT, ident], [tp])
                    dstv = v_tm[p][:, b8 * 8:(b8 + 1) * 8, :, 0:64]
                    srcv = tp[:].rearrange("p (b h e) -> p b h e", b=8, h=2)
                    if b8 % 2 == 0:
                        P.op("act", lambda e, dstv=dstv, srcv=srcv: e.activation(out=dstv, in_=srcv, func=AF.Copy),
                             [tp], [v_tm[p]])
                    else:
                        P.op("dve", lambda e, dstv=dstv, srcv=srcv: e.tensor_copy(out=dstv, in_=srcv), [tp], [v_tm[p]])
            for hh in range(2):
                h = 2 * c + hh
                p0 = hh * 64
                q0 = 64 - p0
                P.op("act", lambda e, c=c: e.activation(out=sq[:], in_=kT[:, c, :], func=AF.Square), [kT], [sq])
                for j in range(8):
                    pm = ps_p.next()
                    P.op("pe", lambda e, j=j, pm=pm, p0=p0: e.matmul(pm[:, :], lhsT=ones_m[p0:p0 + 64, :],
                                                                      rhs=sq[p0:p0 + 64, j * TT:(j + 1) * TT], start=True, stop=True),
                         [ones_m, sq], [pm])
                    P.op("dve", lambda e, j=j, pm=pm: e.reduce_max(out=kmx[:, j:j + 1], in_=pm[:, :],
                                                                   axis=mybir.AxisListType.X), [pm], [kmx])
                P.op("dve", lambda e: e.reduce_max(out=kmx[:, 8:9], in_=kmx[:, 0:8], axis=mybir.AxisListType.X),
                     [kmx], [kmx])
                P.op("dve", lambda e: e.tensor_scalar(out=kmx[:, 8:9], in0=kmx[:, 8:9], scalar1=1.1, scalar2=None,
                                                      op0=ALU.mult), [kmx], [kmx])
                P.op("dve", lambda e, p0=p0, c=c: e.tensor_copy(out=kh[p0:p0 + 64, :], in_=kT[p0:p0 + 64, c, :]), [kT], [kh])
                P.op("dve", lambda e, q0=q0: e.memset(kh[q0:q0 + 64, :], -1.0 / 64.0), [], [kh])
                P.op("act", lambda e, p0=p0, c=c: e.activation(out=qh[p0:p0 + 64, :], in_=qT[p0:p0 + 64, c, :], func=AF.Copy),
                     [qT], [qh])
                P.op("act", lambda e, c=c: e.activation(out=sq[:], in_=qT[:, c, :], func=AF.Square), [qT], [sq])
                for j in range(8):
                    pm = ps_p.next()
                    P.op("pe", lambda e, j=j, pm=pm, p0=p0: e.matmul(pm[:, :], lhsT=ones_m[p0:p0 + 64, :],
                                                                      rhs=sq[p0:p0 + 64, j * TT:(j + 1) * TT], start=True, stop=True),
                         [ones_m, sq], [pm])
                    P.op("act", lambda e, j=j, pm=pm, q0=q0: e.activation(
                        out=lnr[q0:q0 + 64, :], in_=pm[q0:q0 + 64, :], func=AF.Ln, scale=kmx[q0:q0 + 64, 8:9],
                        bias=self.eps_t[q0:q0 + 64, :]), [pm, kmx, self.eps_t], [lnr])
                    P.op("act", lambda e, j=j, q0=q0: e.activation(out=qh[q0:q0 + 64, j * TT:(j + 1) * TT],
                                                                    in_=lnr[q0:q0 + 64, :], func=AF.Exp, scale=0.5), [lnr], [qh])
                for p, dil in enumerate(dils):
                    nb = 32 // dil
                    qv = qh[:, :].rearrange("p (l r) -> p r l", r=dil)
                    kv = kh[:, :].rearrange("p (l r) -> p r l", r=dil)
                    av = acc[:, :].rearrange("p (l r) -> p r l", r=dil)
                    blocks = [(r, n) for r in range(dil) for n in range(nb)]

                    def mb_scores(r, n, p=p, nb=nb, kv=kv, qv=qv, h=h):
                        W = 256 if n < nb - 1 else 128
                        ps_ = ps_p.next()
                        P.op("pe", lambda e: e.matmul(
                            ps_[:, 0:W], lhsT=kv[:, r, n * 128:(n + 1) * 128], rhs=qv[:, r, n * 128:n * 128 + W],
                            start=True, stop=False), [kh, qh], [ps_])
                        P.op("pe", lambda e: e.matmul(
                            ps_[:, 0:W], lhsT=ident[:], rhs=biasT[:, p * 4 + h, 0:W], start=False, stop=True),
                            [ident, biasT], [ps_])
                        pT = pT_p.next()
                        P.op("act", lambda e: e.activation(out=pT[:, 0:W], in_=ps_[:, 0:W], func=AF.Exp), [ps_], [pT])
                        return pT

                    state = {"pq_next": None}

                    def mb_pv(r, n, pT, p=p, nb=nb, av=av, hh=hh):
                        b = r * nb + n
                        pq_cur = state["pq_next"] if n > 0 else pq_p.next()
                        P.op("pe", lambda e: e.matmul(pq_cur[:, 0:128], lhsT=v_tm[p][:, b, hh, :], rhs=pT[:, 0:128],
                                                      start=(n == 0), stop=True), [v_tm[p], pT], [pq_cur])
                        dst = av[:, r, n * 128:(n + 1) * 128]
                        if p == 0:
                            P.op("dve", lambda e: e.tensor_copy(out=dst, in_=pq_cur[:, 0:128]), [pq_cur], [acc])
                        else:
                            P.op("dve", lambda e: e.tensor_tensor(out=dst, in0=dst, in1=pq_cur[:, 0:128], op=ALU.add),
                                 [pq_cur, acc], [acc])
                        if n < nb - 1:
                            pq_next = pq_p.next()
                            P.op("pe", lambda e: e.matmul(pq_next[:, 0:128], lhsT=v_tm[p][:, b, hh, :], rhs=pT[:, 128:256],
                                                          start=True, stop=False), [v_tm[p], pT], [pq_next])
                            state["pq_next"] = pq_next

                    pT_cur = mb_scores(*blocks[0])
                    for i, (r, n) in enumerate(blocks):
                        pT_nxt = mb_scores(*blocks[i + 1]) if i + 1 < len(blocks) else None
                        mb_pv(r, n, pT_cur)
                        pT_cur = pT_nxt
                for j in range(8):
                    pm = ps_p.next()
                    P.op("pe", lambda e, j=j, pm=pm: e.matmul(pm[0:64, :], lhsT=ident_f[:, 64:128], rhs=acc[:, j * TT:(j + 1) * TT],
                                                              start=True, stop=True), [ident_f, acc], [pm])
                    rden = rden_p.next()
                    P.op("act", lambda e, pm=pm, rden=rden: e.activation(out=rden[0:64, :], in_=pm[0:64, :], func=AF.Ln),
                         [pm], [rden])
                    P.op("act", lambda e, rden=rden: e.activation(out=rden[0:64, :], in_=rden[0:64, :], func=AF.Exp, scale=-1.0),
                         [rden], [rden])
                    P.op("dve", lambda e, j=j, rden=rden: e.tensor_tensor(out=cst[0:64, j * TT:(j + 1) * TT],
                                                                          in0=acc[0:64, j * TT:(j + 1) * TT], in1=rden[0:64, :],
                                                                          op=ALU.mult), [acc, rden], [cst])
                P.dma(self.c_s[h], cst[0:64, :], reads=[cst], writes=[self.k_c])
        P.barrier()
        P.release(m0)

    def mixer_c(self, l, src, dst):
        P = self.P
        kp = self.kprep[(l, "mix")]
        m0 = P.mark()
        w_o = P.sbuf("w_o", [128, 8, D], BF16)
        P.dma(w_o[:], self.wout_s[l].rearrange("(k p) n -> p k n", p=128), reads=[kp], writes=[w_o])
        gpost = P.sbuf("gpost", [128, D], F32)
        self.load_gain(gpost, self.norm_post[l, 1])
        hT_p = P.pool("hT", [128, 4, D], F32, 2)
        M_p = P.pool("Mst", [128, 8, TT], BF16, 2)
        ytmp = P.sbuf("ytmp", [128, D], F32)
        ss2 = P.sbuf("ss2", [128, 8], F32)
        rstd2 = P.sbuf("rstd2", [128, 4], F32)
        py_p = P.pool("py", [128, TT], F32, 4, psum=True)

        def loads(t):
            hT, M = hT_p.next(), M_p.next()
            P.dma(hT[:], src[t * TT:(t + 1) * TT, :].rearrange("(s p) d -> p s d", p=128), reads=[self.k_h], writes=[hT])
            P.dma(M[:, 0:2, :], self.a_s[:, :, t * TT:(t + 1) * TT].rearrange("(j g) p t -> (g p) j t", g=2),
                  reads=[self.k_ab], writes=[M])
            P.dma(M[:, 2:6, :], self.b_s[:, :, t * TT:(t + 1) * TT].rearrange("g p t -> p g t"), reads=[self.k_ab], writes=[M])
            P.dma(M[:, 6:8, :], self.c_s[:, :, t * TT:(t + 1) * TT].rearrange("(j g) p t -> (g p) j t", g=2),
                  reads=[self.k_c], writes=[M])
            return hT, M

        nxt = loads(0)
        for t in range(NT):
            hT, M = nxt
            if t + 1 < NT:
                nxt = loads(t + 1)
            self.back(dst, t, hT, M, 8, lambda k, n: w_o[:, k, n * 512:(n + 1) * 512], [w_o], gpost, py_p, ytmp, ss2, rstd2)
        P.barrier()
        P.release(m0)

    def copy_phase(self, src, dst):
        P = self.P
        m = P.mark()
        hT_p = P.pool("hT", [128, 4, D], F32, 2)
        for t in range(NT):
            hT = hT_p.next()
            P.dma(hT[:], src[t * TT:(t + 1) * TT, :].rearrange("(s p) d -> p s d", p=128), reads=[self.k_h], writes=[hT])
            kd = self.k_out if dst is self.out else self.k_h
            P.dma(dst[t * TT:(t + 1) * TT, :].rearrange("(s p) d -> p s d", p=128), hT[:], reads=[hT], writes=[kd])
        P.barrier()
        P.release(m)

    def build(self):
        P = self.P
        self.declare()
        plan = self.plan
        if plan is None:
            plan = [(l, s) for l in range(self.nl) for s in range(4)]
        layers = sorted(set(l for l, _ in plan))
        for l in layers:
            subs = set(s for ll, s in plan if ll == l)
            which = []
            if 0 in subs or 3 in subs or 10 in subs:
                which.append("ffn")
            if 1 in subs:
                which.append("mix")
            if 2 in subs:
                which.append("xa")
            self.prep_weights(l, which)
        self.consts()
        cur = self.x
        for i, (l, s) in enumerate(plan):
            dst = self.out if i == len(plan) - 1 else self.h
            if s >= 9:
                self.copy_phase(cur, dst)
            elif s == 0:
                self.ffn(l, 0, cur, dst)
            elif s == 3:
                self.ffn(l, 1, cur, dst)
            elif s == 1:
                self.mixer(l, cur, dst)
            elif s == 2:
                self.xattn(l, cur, dst)
            cur = dst
        P.op("sp", lambda e: e.nop(), reads=[self.k_out], writes=[])
        P.finish()
        return self.nc


def _t5_bucket_np(dist):
    dist = np.asarray(dist)
    max_exact = 16
    d = np.maximum(dist, 1).astype(np.float32)
    large = max_exact + (np.log(d / np.float32(max_exact)) / np.float32(math.log(2048 / max_exact))
                         * np.float32(32 - max_exact)).astype(np.int32)
    large = np.minimum(large, 31)
    return np.where(dist < max_exact, dist, large)


def host_consts(rel_bias):
    j = np.arange(128)[:, None]
    i = np.arange(128)[None, :]
    relb = np.zeros((3, 4, 128, 256), np.float32)
    amask = np.zeros((128, 256), np.float32)
    for p, dil in enumerate((1, 4, 16)):
        d_cur = np.maximum(i - j, 0)
        d_prev = np.clip(i - j + 128, 0, 128)
        b_cur = _t5_bucket_np(d_cur * dil)
        b_prev = _t5_bucket_np(d_prev * dil)
        for hh in range(4):
            relb[p, hh, :, 0:128] = rel_bias[b_cur, hh]
            relb[p, hh, :, 128:256] = rel_bias[b_prev, hh]
    amask[:, 0:128] = np.where(i >= j, 0.0, -30000.0)
    amask[:, 128:256] = np.where(i <= j, 0.0, -30000.0)
    ident = np.eye(128, dtype=np.float32)
    tri = np.zeros((3, 128, 128), np.float32)
    a = np.arange(128)[:, None]
    b = np.arange(128)[None, :]
    tri[0] = (a <= b)
    tri[1] = (a > b)
    tri[2] = 1.0
    return dict(relb=relb, amask=amask, c_ident=ident, c_tri=tri)


_CACHE = {}


def kernel(**inputs):
    inputs = {k: np.asarray(v) for k, v in inputs.items()}
    if "nc" not in _CACHE:
        _CACHE["nc"] = Builder().build()
    nc = _CACHE["nc"]
    hc = host_consts(inputs["rel_bias"].astype(np.float32))
    shared = {k: np.ascontiguousarray(v, dtype=np.float32) for k, v in inputs.items()
              if k not in ("x", "mem", "rel_bias")}
    shared.update(hc)
    in_maps = []
    for b in range(8):
        m = dict(shared)
        m["x"] = np.ascontiguousarray(inputs["x"][b], dtype=np.float32)
        m["mem"] = np.ascontiguousarray(inputs["mem"][b], dtype=np.float32)
        in_maps.append(m)
    res = run_bass_kernel_spmd(nc, in_maps, core_ids=list(range(8)))
    return np.stack([np.asarray(r["out"], dtype=np.float32) for r in res.results], axis=0)
```

```python
import math
import numpy as np
import ml_dtypes
import concourse.bass as bass
import concourse.mybir as mybir
from concourse.bass_utils import run_bass_kernel_spmd

F32 = mybir.dt.float32
BF16 = mybir.dt.bfloat16
AF = mybir.ActivationFunctionType
ALU = mybir.AluOpType

ENGS = ("pe", "act", "dve", "pool", "sp")

D = 1024
S = 4096
DEPTH = 4
DFF = 2816
NIN = 2824
MEM = 256
EPS = 1e-6
TT = 512
NT = S // TT
NSLAB = 6


class Buf:
    __slots__ = ("name", "t", "last_w", "readers", "dsem", "dcount")

    def __init__(self, name, t=None):
        self.name = name
        self.t = t
        self.last_w = None
        self.readers = []
        self.dsem = None
        self.dcount = 0

    def __getitem__(self, key):
        return self.t[key]


class Op:
    __slots__ = ("eng", "fn", "reads", "writes", "dma", "deps", "idx", "sig", "dbuf", "dval", "extra")

    def __init__(self, eng, fn, reads, writes, dma=False):
        self.eng = eng
        self.fn = fn
        self.reads = reads
        self.writes = writes
        self.dma = dma
        self.deps = []
        self.sig = None
        self.dbuf = None
        self.dval = None
        self.extra = []


class Pool:
    def __init__(self, bufs):
        self.bufs = bufs
        self.i = 0

    def next(self):
        b = self.bufs[self.i % len(self.bufs)]
        self.i += 1
        return b


class Prog:
    def __init__(self, nc):
        self.nc = nc
        self.ops = []
        self._ctx = []
        self._nm = 0
        self.pending = {e: [] for e in ENGS}
        self.sbuf_dmas = []

    def sbuf(self, name, shape, dtype):
        self._nm += 1
        g = self.nc.sbuf_tensor(f"{name}_{self._nm}", list(shape), dtype)
        t = g.__enter__()
        b = Buf(name, t)
        self._ctx.append((g, b))
        return b

    def psum(self, name, shape, dtype=F32):
        self._nm += 1
        g = self.nc.psum_tensor(f"{name}_{self._nm}", list(shape), dtype)
        t = g.__enter__()
        b = Buf(name, t)
        self._ctx.append((g, b))
        return b

    def pool(self, name, shape, dtype, n, psum=False):
        f = self.psum if psum else self.sbuf
        return Pool([f(f"{name}{i}", shape, dtype) for i in range(n)])

    def key(self, name):
        return Buf(name, None)

    def mark(self):
        return len(self._ctx)

    def release(self, mark, final=False):
        rel = []
        while len(self._ctx) > mark:
            g, b = self._ctx.pop()
            g.__exit__(None, None, None)
            rel.append(b)
        if not final:
            o = Op(None, None, [], [], False)
            o.idx = len(self.ops)
            o.extra = rel
            self.ops.append(o)

    def op(self, eng, fn, reads=(), writes=(), dma=False):
        o = Op(eng, fn, list(reads), list(writes), dma)
        o.idx = len(self.ops)
        if self.pending[eng]:
            o.extra = self.pending[eng]
            self.pending[eng] = []
        self.ops.append(o)
        return o

    def dma(self, out, in_, reads=(), writes=(), q="sp", dbuf=None, **kw):
        o = self.op(q, lambda e: e.dma_start(out=out, in_=in_, **kw), reads, writes, dma=True)
        o.dbuf = dbuf
        if any(b.t is not None for b in o.reads + o.writes):
            self.sbuf_dmas.append(o)
        return o

    def barrier(self):
        last = {}
        for o in self.ops:
            if o.eng is not None and not o.dma:
                last[o.eng] = o
        deps = list(last.values()) + list(self.sbuf_dmas)
        self.sbuf_dmas = []
        for e in ENGS:
            self.pending[e] = self.pending[e] + deps

    def analyze(self, upto=None):
        pass

    def finish(self):
        nc = self.nc
        ops = self.ops
        for o in ops:
            if o.eng is None:
                continue
            deps = {}
            for b in o.reads:
                if b.last_w is not None:
                    deps[b.last_w.idx] = ("raw", b.last_w)
            for b in o.writes:
                if b.last_w is not None and b.last_w.idx not in deps:
                    deps[b.last_w.idx] = ("waw", b.last_w)
                for r in b.readers:
                    if r.idx not in deps:
                        deps[r.idx] = ("war", r)
            for b in o.reads:
                b.readers.append(o)
            for b in o.writes:
                b.last_w = o
                b.readers = []
            keep = []
            for kind, d in deps.values():
                if d is o:
                    continue
                if d.eng == o.eng and not d.dma and not o.dma:
                    if o.eng == "pe":
                        continue
                    if kind != "raw":
                        continue
                keep.append(d)
            seen = set(id(d) for d in keep)
            for d in o.extra:
                if id(d) in seen or d is o:
                    continue
                if d.eng == o.eng and not d.dma and not o.dma:
                    continue
                seen.add(id(d))
                keep.append(d)
            o.deps = keep
        need = set()
        for o in ops:
            if o.eng is None:
                continue
            if o.dma:
                need.add(o.idx)
            for d in o.deps:
                need.add(d.idx)
        cnt = {e: 0 for e in ENGS}
        sem_ctx = []

        def mksem(name):
            g = nc.semaphore(name)
            s = g.__enter__()
            sem_ctx.append(g)
            return s

        esem = {}
        nd = 0
        free_sems = []
        for o in ops:
            if o.eng is None:
                for b in o.extra:
                    if b.dsem is not None:
                        free_sems.append(b.dsem)
                continue
            if o.idx not in need:
                continue
            if o.dma:
                b = o.dbuf
                if b is None:
                    cands = [x for x in (o.writes + o.reads) if x.t is not None]
                    b = cands[0] if cands else (o.writes + o.reads)[0]
                if b.dsem is None:
                    if free_sems:
                        b.dsem = free_sems.pop()
                    else:
                        nd += 1
                        b.dsem = [mksem(f"d{nd}"), 0]
                b.dsem[1] += 16
                o.dbuf = b
                o.dval = b.dsem[1]
            else:
                if o.eng not in esem:
                    esem[o.eng] = mksem("e_" + o.eng)
                cnt[o.eng] += 1
                o.sig = cnt[o.eng]
        self.sig_counts = dict(cnt)
        self.n_dsem = nd
        for o in ops:
            if o.eng is not None and o.dma:
                o.dbuf = type("D", (), {"dsem": o.dbuf.dsem})()
        by_eng = {e: [o for o in ops if o.eng == e] for e in ENGS}
        with nc.Block() as block:
            def emit(engname, eng):
                waited = {}
                for o in by_eng[engname]:
                    for d in o.deps:
                        if d.dma:
                            k = ("d", id(d.dbuf.dsem))
                            sem, val = d.dbuf.dsem[0], d.dval
                        else:
                            k = ("e", d.eng)
                            sem, val = esem[d.eng], d.sig
                        if waited.get(k, 0) >= val:
                            continue
                        waited[k] = val
                        eng.wait_ge(sem, val)
                    ins = o.fn(eng)
                    if o.idx in need:
                        if o.dma:
                            ins.then_inc(o.dbuf.dsem[0], 16)
                        else:
                            ins.then_inc(esem[o.eng], 1)

            if by_eng["sp"]:
                block.sync(lambda e: emit("sp", e))
            if by_eng["act"]:
                block.scalar(lambda e: emit("act", e))
            if by_eng["dve"]:
                block.vector(lambda e: emit("dve", e))
            if by_eng["pool"]:
                block.gpsimd(lambda e: emit("pool", e))
            if by_eng["pe"]:
                block.tensor(lambda e: emit("pe", e))
        for g in reversed(sem_ctx):
            g.__exit__(None, None, None)
        self.release(0, final=True)


class Builder:
    def __init__(self, nlayers=DEPTH, plan=None, debug=False):
        self.debug = debug
        self.nl = nlayers
        self.plan = plan
        nc = bass.Bass("TRN2", target_bir_lowering=False)
        self.nc = nc
        self.P = Prog(nc)
        self.dram = {}
        self.allkeys = []

    def din(self, name, shape, dtype=F32):
        ap = self.nc.dram_tensor(name, list(shape), dtype, kind="ExternalInput").ap()
        self.dram[name] = ap
        return ap

    def dscr(self, name, shape, dtype):
        kind = "ExternalOutput" if (self.debug and name in ("a_s", "b_s", "c_s", "qkv_s")) else "Internal"
        ap = self.nc.dram_tensor(name, list(shape), dtype, kind=kind).ap()
        self.dram[name] = ap
        return ap

    def key(self, name):
        k = self.P.key(name)
        self.allkeys.append(k)
        return k

    def declare(self):
        nl = DEPTH
        self.x = self.din("x", [S, D])
        self.mem = self.din("mem", [MEM, D])
        self.norm_pre = self.din("norm_pre", [nl, 4, D])
        self.norm_post = self.din("norm_post", [nl, 4, D])
        self.ffn_wi = self.din("ffn_wi", [nl, 2, D, 2 * DFF])
        self.ffn_wo = self.din("ffn_wo", [nl, 2, DFF, D])
        self.mix_w_in = self.din("mix_w_in", [nl, D, NIN])
        self.mix_w_out = self.din("mix_w_out", [nl, D, D])
        self.sgu_ln_g = self.din("sgu_ln_g", [nl, 256])
        self.sgu_w = self.din("sgu_w", [nl, 4, 128, 128])
        self.sgu_b = self.din("sgu_b", [nl, 4, 128])
        self.conv_w = self.din("ssm_conv_w", [nl, 4, 1024])
        self.conv_b = self.din("ssm_conv_b", [nl, 1024])
        self.dt_bias = self.din("ssm_dt_bias", [nl, 8])
        self.a_log = self.din("ssm_a_log", [nl, 8])
        self.ssm_d = self.din("ssm_d", [nl, 8])
        self.ssm_norm_g = self.din("ssm_norm_g", [nl, 512])
        self.relb = self.din("relb", [3, 4, 128, 256])
        self.amask = self.din("amask", [128, 256])
        self.mem_norm_g = self.din("mem_norm_g", [nl, D])
        self.wq = self.din("xattn_wq", [nl, D, D])
        self.wkv = self.din("xattn_wkv", [nl, D, 2 * D])
        self.wo = self.din("xattn_wo", [nl, D, D])
        self.c_ident = self.din("c_ident", [128, 128])
        self.c_tri = self.din("c_tri", [3, 128, 128])
        self.out = self.nc.dram_tensor("out", [S, D], F32, kind="ExternalOutput").ap()
        self.h = self.dscr("h_scr", [S, D], F32)
        self.wi_s = self.dscr("wi_s", [nl, 2, NSLAB, 128, 8, 2, 512], BF16)
        self.wo_s = self.dscr("wo_s", [nl, 2, DFF, D], BF16)
        self.win_s = self.dscr("win_s", [nl, D, NIN], BF16)
        self.wout_s = self.dscr("wout_s", [nl, D, D], BF16)
        self.wq_s = self.dscr("wq_s", [nl, D, D], BF16)
        self.wkv_s = self.dscr("wkv_s", [nl, D, 2 * D], BF16)
        self.wox_s = self.dscr("wox_s", [nl, D, D], BF16)
        self.a_s = self.dscr("a_s", [4, 64, S], BF16)
        self.b_s = self.dscr("b_s", [4, 128, S], BF16)
        self.c_s = self.dscr("c_s", [4, 64, S], BF16)
        self.qkv_s = self.dscr("qkv_s", [6, 128, S], BF16)
        self.k_ab = self.key("ab")
        self.k_qkv = self.key("qkv")
        self.k_c = self.key("c")
        self.k_h = self.key("h")
        self.k_out = self.key("out")
        self.k_prep = self.P.key("prep")
        self.kprep = {}

    def prep_weights(self, l, which=("ffn", "mix", "xa")):
        P = self.P
        kp = self.k_prep
        grp = [None]

        def cast(out, in_):
            P.dma(out, in_, writes=[kp, grp[0]], q="pool", dbuf=kp)

        if "ffn" in which:
            for f in range(2):
                grp[0] = self.kprep[(l, "ffn%d" % f)] = P.key("prep_ffn")
                wi = self.ffn_wi[l, f].rearrange("(k p) n -> p k n", p=128)
                for jg in range(NSLAB):
                    w = 512 if jg < 5 else 256
                    for t in range(2):
                        c0 = t * DFF + jg * 512
                        cast(self.wi_s[l, f, jg, :, :, t, 0:w], wi[:, :, c0:c0 + w])
                for r0 in range(0, DFF, 704):
                    cast(self.wo_s[l, f, r0:r0 + 704, :], self.ffn_wo[l, f, r0:r0 + 704, :])
        if "mix" in which:
            grp[0] = self.kprep[(l, "mix")] = P.key("prep_mix")
            for r0 in range(0, D, 512):
                cast(self.win_s[l, r0:r0 + 512, :], self.mix_w_in[l, r0:r0 + 512, :])
            cast(self.wout_s[l], self.mix_w_out[l])
        if "xa" in which:
            grp[0] = self.kprep[(l, "xa")] = P.key("prep_xa")
            cast(self.wq_s[l], self.wq[l])
            for r0 in range(0, D, 512):
                cast(self.wkv_s[l, r0:r0 + 512, :], self.wkv[l, r0:r0 + 512, :])
            cast(self.wox_s[l], self.wo[l])

    def consts(self):
        P = self.P
        self.ident_f = P.sbuf("ident_f", [128, 128], F32)
        self.ident = P.sbuf("ident", [128, 128], BF16)
        P.dma(self.ident_f[:], self.c_ident[:, :], writes=[self.ident_f])
        P.op("dve", lambda e: e.tensor_copy(out=self.ident[:], in_=self.ident_f[:]), [self.ident_f], [self.ident])
        self.junk = P.sbuf("junk", [128, 1024], BF16)
        self.eps_t = P.sbuf("eps_t", [128, 1], F32)
        self.biasT = P.sbuf("biasT", [128, 12, 256], BF16)
        self.ones_m = P.sbuf("ones_m", [128, 128], BF16)
        self.negones = P.sbuf("negones", [128, 128], BF16)
        P.op("dve", lambda e: e.memset(self.ones_m[:], 1.0), [], [self.ones_m])
        P.op("dve", lambda e: e.memset(self.negones[:], -1.0), [], [self.negones])
        self.neg128 = P.sbuf("neg128", [128, 128], BF16)
        P.op("dve", lambda e: e.memset(self.neg128[:], -1.0 / 128.0), [], [self.neg128])
        m = P.mark()
        relb_f = P.sbuf("relb_f", [128, 12, 256], F32)
        am = P.sbuf("am", [128, 256], F32)
        P.dma(relb_f[:], self.relb.rearrange("p h j i -> j (p h) i"), writes=[relb_f])
        P.dma(am[:], self.amask[:, :], writes=[am])
        P.op("dve", lambda e: e.tensor_tensor(out=relb_f[:], in0=relb_f[:], in1=am[:].unsqueeze(1).to_broadcast([128, 12, 256]),
                                              op=ALU.add), [relb_f, am], [relb_f])
        P.op("dve", lambda e: e.tensor_copy(out=self.biasT[:], in_=relb_f[:]), [relb_f], [self.biasT])
        P.barrier()
        P.release(m)
        P.op("dve", lambda e: e.memset(self.eps_t[:], EPS), [], [self.eps_t])

    def load_gain(self, buf, src_row, scale=None):
        P = self.P
        P.dma(buf[:], src_row.partition_broadcast(128), writes=[buf])
        if scale is not None:
            P.op("dve", lambda e: e.tensor_scalar(out=buf[:], in0=buf[:], scalar1=float(scale), scalar2=None,
                                                  op0=ALU.mult), [buf], [buf])

    def front(self, src, t, hT, gpre, hn, hnT, ss, rstd, tp_pool, use_ln=False):
        self.front_a(src, t, hT, gpre, hn, ss, rstd, use_ln=use_ln)
        self.front_b(hn, hnT, tp_pool)

    def rsqrt_lnexp(self, out_ap, in_ap, scale, bufs_r, bufs_w):
        P = self.P
        P.op("act", lambda e: e.activation(out=out_ap, in_=in_ap, func=AF.Ln, scale=scale, bias=self.eps_t[:]),
             list(bufs_r) + [self.eps_t], list(bufs_w))
        P.op("act", lambda e: e.activation(out=out_ap, in_=out_ap, func=AF.Exp, scale=-0.5), list(bufs_w), list(bufs_w))

    def front_a(self, src, t, hT, gpre, hn, ss, rstd, use_ln=False):
        P = self.P
        junk = self.junk
        tile = src[t * TT:(t + 1) * TT, :].rearrange("(s p) d -> p s d", p=128)
        P.dma(hT[:], tile, reads=[self.k_h], writes=[hT])
        for s in range(4):
            P.op("act", lambda e, s=s: e.activation(out=junk[:], in_=hT[:, s, :], func=AF.Square,
                                                     accum_out=ss[:, s:s + 1]), [hT], [junk, ss])
        if use_ln:
            self.rsqrt_lnexp(rstd[:], ss[:], 1.0 / D, [ss], [rstd])
        else:
            P.op("act", lambda e: e.activation(out=rstd[:], in_=ss[:], func=AF.Sqrt, scale=1.0 / D, bias=self.eps_t[:]),
                 [ss, self.eps_t], [rstd])
            P.op("dve", lambda e: e.reciprocal(out=rstd[:], in_=rstd[:]), [rstd], [rstd])
        for s in range(4):
            P.op("dve", lambda e, s=s: e.scalar_tensor_tensor(out=hn[:, s, :], in0=hT[:, s, :], scalar=rstd[:, s:s + 1],
                                                               in1=gpre[:], op0=ALU.mult, op1=ALU.mult),
                 [hT, rstd, gpre], [hn])

    def front_b(self, hn, hnT, tp_pool):
        P = self.P
        ident = self.ident
        for c2 in range(4):
            tp = tp_pool.next()
            for cc in range(2):
                c = 2 * c2 + cc
                for s in range(4):
                    P.op("pe", lambda e, s=s, c=c, cc=cc, tp=tp: e.transpose(
                        tp[:, cc * 512 + s * 128:cc * 512 + (s + 1) * 128], hn[:, s, c * 128:(c + 1) * 128], ident[:]),
                        [hn, ident], [tp])
            src = tp[:].rearrange("p (c t) -> p c t", c=2)
            if c2 % 2 == 0:
                P.op("act", lambda e, c2=c2, src=src: e.activation(out=hnT[:, 2 * c2:2 * c2 + 2, :], in_=src, func=AF.Copy),
                     [tp], [hnT])
            else:
                P.op("dve", lambda e, c2=c2, src=src: e.tensor_copy(out=hnT[:, 2 * c2:2 * c2 + 2, :], in_=src), [tp], [hnT])

    def back(self, dst, t, hT, actT, nk, w_rhs, w_bufs, gpost, py_pool, ytmp, ss2, rstd2, lhs_fn=None, lhs_bufs=None,
             use_ln=False):
        P = self.P
        junk = self.junk
        if lhs_fn is None:
            lhs_fn = lambda k, s: actT[:, k, s * 128:(s + 1) * 128]
            lhs_bufs = [actT]
        P.op("dve", lambda e: e.memset(ss2[:], 0.0), [], [ss2])
        for s in range(4):
            for n in range(2):
                py = py_pool.next()
                for k in range(nk):
                    P.op("pe", lambda e, s=s, n=n, k=k, py=py: e.matmul(
                        py[:], lhsT=lhs_fn(k, s), rhs=w_rhs(k, n),
                        start=(k == 0), stop=(k == nk - 1)), list(lhs_bufs) + list(w_bufs), [py])
                P.op("act", lambda e, s=s, n=n, py=py: e.activation(out=junk[:, 0:512], in_=py[:], func=AF.Square,
                                                                     accum_out=ss2[:, 2 * s + n:2 * s + n + 1]),
                     [py], [junk, ss2, py])
                P.op("dve", lambda e, n=n, py=py: e.tensor_copy(out=ytmp[:, n * 512:(n + 1) * 512], in_=py[:]),
                     [py], [ytmp])
            P.op("dve", lambda e, s=s: e.tensor_tensor(out=rstd2[:, s:s + 1], in0=ss2[:, 2 * s:2 * s + 1],
                                                       in1=ss2[:, 2 * s + 1:2 * s + 2], op=ALU.add), [ss2], [rstd2])
            if use_ln:
                self.rsqrt_lnexp(rstd2[:, s:s + 1], rstd2[:, s:s + 1], 1.0 / D, [rstd2], [rstd2])
            else:
                P.op("act", lambda e, s=s: e.activation(out=rstd2[:, s:s + 1], in_=rstd2[:, s:s + 1], func=AF.Sqrt,
                                                         scale=1.0 / D, bias=self.eps_t[:]), [rstd2, self.eps_t], [rstd2])
                P.op("dve", lambda e, s=s: e.reciprocal(out=rstd2[:, s:s + 1], in_=rstd2[:, s:s + 1]), [rstd2], [rstd2])
            P.op("dve", lambda e, s=s: e.scalar_tensor_tensor(out=ytmp[:], in0=ytmp[:], scalar=rstd2[:, s:s + 1],
                                                               in1=gpost[:], op0=ALU.mult, op1=ALU.mult),
                 [ytmp, rstd2, gpost], [ytmp])
            P.op("dve", lambda e, s=s: e.tensor_tensor(out=hT[:, s, :], in0=hT[:, s, :], in1=ytmp[:], op=ALU.add),
                 [hT, ytmp], [hT])
        tile = dst[t * TT:(t + 1) * TT, :].rearrange("(s p) d -> p s d", p=128)
        kd = self.k_out if dst is self.out else self.k_h
        P.dma(tile, hT[:], reads=[hT], writes=[kd])

    def ffn(self, l, f, src, dst):
        P = self.P
        m = P.mark()
        sub = 0 if f == 0 else 3
        gpre = P.sbuf("gpre", [128, D], F32)
        gpost = P.sbuf("gpost", [128, D], F32)
        self.load_gain(gpre, self.norm_pre[l, sub])
        self.load_gain(gpost, self.norm_post[l, sub], scale=0.5)
        wo = P.sbuf("wo", [128, 22, D], BF16)
        hT_p = P.pool("hT", [128, 4, D], F32, 2)
        hn = P.sbuf("hn", [128, 4, D], BF16)
        hnT_p = P.pool("hnT", [128, 8, TT], BF16, 2)
        actT = P.sbuf("actT", [128, 22, TT], BF16)
        sg_p = P.pool("sg", [128, TT], F32, 2)
        ytmp = P.sbuf("ytmp", [128, D], F32)
        ss = P.sbuf("ss", [128, 4], F32)
        rstd = P.sbuf("rstd", [128, 4], F32)
        ss2 = P.sbuf("ss2", [128, 8], F32)
        rstd2 = P.sbuf("rstd2", [128, 4], F32)
        slab_p = P.pool("slab", [128, 8, 2, 512], BF16, 3)
        tp_p = P.pool("tp", [128, 2 * TT], BF16, 2, psum=True)
        pg_p = P.pool("pg", [128, TT], F32, 2, psum=True)
        pu_p = P.pool("pu", [128, TT], F32, 2, psum=True)
        py_p = P.pool("py", [128, TT], F32, 2, psum=True)

        slabs = {}
        nload = [0]

        def ensure(i):
            while nload[0] <= i and nload[0] < NT * NSLAB:
                jg_ = nload[0] % NSLAB
                sl = slab_p.next()
                P.dma(sl[:], self.wi_s[l, f, jg_], reads=[self.kprep[(l, "ffn%d" % f)]], writes=[sl])
                slabs[nload[0]] = sl
                nload[0] += 1

        hTs = [None] * NT
        hnTs = [None] * NT
        hTs[0] = hT_p.next()
        hnTs[0] = hnT_p.next()
        self.front(src, 0, hTs[0], gpre, hn, hnTs[0], ss, rstd, tp_p)
        ensure(1)
        P.dma(wo[:], self.wo_s[l, f].rearrange("(k p) n -> p k n", p=128), reads=[self.kprep[(l, "ffn%d" % f)]], writes=[wo])
        for t in range(NT):
            hT, hnT = hTs[t], hnTs[t]
            for jg in range(NSLAB):
                ensure(t * NSLAB + jg + 2)
                sl = slabs.pop(t * NSLAB + jg)
                nj = 4 if jg < 5 else 2
                for jj in range(nj):
                    j = jg * 4 + jj
                    pg, pu = pg_p.next(), pu_p.next()
                    for k in range(8):
                        P.op("pe", lambda e, k=k, jj=jj, pg=pg, sl=sl, hnT=hnT: e.matmul(
                            pg[:], lhsT=sl[:, k, 0, jj * 128:(jj + 1) * 128], rhs=hnT[:, k, :],
                            start=(k == 0), stop=(k == 7)), [sl, hnT], [pg])
                    for k in range(8):
                        P.op("pe", lambda e, k=k, jj=jj, pu=pu, sl=sl, hnT=hnT: e.matmul(
                            pu[:], lhsT=sl[:, k, 1, jj * 128:(jj + 1) * 128], rhs=hnT[:, k, :],
                            start=(k == 0), stop=(k == 7)), [sl, hnT], [pu])
                    sg = sg_p.next()
                    P.op("act", lambda e, pg=pg, sg=sg: e.activation(out=sg[:], in_=pg[:], func=AF.Silu), [pg], [sg])
                    P.op("dve", lambda e, j=j, pu=pu, sg=sg: e.tensor_tensor(out=actT[:, j, :], in0=pu[:], in1=sg[:],
                                                                              op=ALU.mult), [pu, sg], [actT])
                if jg == 1 and t + 1 < NT:
                    hTs[t + 1] = hT_p.next()
                    hnTs[t + 1] = hnT_p.next()
                    self.front_a(src, t + 1, hTs[t + 1], gpre, hn, ss, rstd)
            if t + 1 < NT:
                self.front_b(hn, hnTs[t + 1], tp_p)
            self.back(dst, t, hT, actT, 22, lambda k, n: wo[:, k, n * 512:(n + 1) * 512], [wo], gpost,
                      py_p, ytmp, ss2, rstd2)
        P.barrier()
        P.release(m)

    def xattn(self, l, src, dst):
        P = self.P
        ident, junk, ones_m, negones = self.ident, self.junk, self.ones_m, self.negones
        kp = self.kprep[(l, "xa")]
        m0 = P.mark()
        gpre = P.sbuf("gpre", [128, D], F32)
        gpost = P.sbuf("gpost", [128, D], F32)
        self.load_gain(gpre, self.norm_pre[l, 2])
        self.load_gain(gpost, self.norm_post[l, 2])
        wq = P.sbuf("wq", [128, 8, D], BF16)
        wox = P.sbuf("wox", [128, 8, D], BF16)
        kT = P.sbuf("kT", [128, 8, MEM], BF16)
        v_tm = P.sbuf("v_tm", [128, 2, D], BF16)
        kmax2 = P.sbuf("kmax2", [128, 4], F32)
        tp_p = P.pool("tp", [128, 2 * TT], BF16, 2, psum=True)
        pa_p = P.pool("pa", [128, TT], F32, 3, psum=True)
        po_p = P.pool("po", [128, TT], F32, 2, psum=True)
        pd_p = P.pool("pd", [128, TT], F32, 1, psum=True)
        ss = P.sbuf("ss", [128, 4], F32)
        rstd = P.sbuf("rstd", [128, 4], F32)
        m1 = P.mark()
        wkv = P.sbuf("wkv", [128, 8, 2 * D], BF16)
        P.dma(wkv[:], self.wkv_s[l].rearrange("(k p) n -> p k n", p=128), reads=[kp], writes=[wkv])
        gmem = P.sbuf("gmem", [128, D], F32)
        self.load_gain(gmem, self.mem_norm_g[l])
        mT = P.sbuf("mT", [128, 2, D], F32)
        memn = P.sbuf("memn", [128, 2, D], BF16)
        memnT = P.sbuf("memnT", [128, 8, MEM], BF16)
        ksqT = P.sbuf("ksqT", [128, 8, MEM], BF16)
        P.dma(mT[:], self.mem.rearrange("(s p) d -> p s d", p=128), writes=[mT])
        P.dma(wq[:], self.wq_s[l].rearrange("(k p) n -> p k n", p=128), reads=[kp], writes=[wq])
        P.dma(wox[:], self.wox_s[l].rearrange("(k p) n -> p k n", p=128), reads=[kp], writes=[wox])
        for s_ in range(2):
            P.op("act", lambda e, s_=s_: e.activation(out=junk[:], in_=mT[:, s_, :], func=AF.Square,
                                                       accum_out=ss[:, s_:s_ + 1]), [mT], [junk, ss])
        self.rsqrt_lnexp(rstd[:, 0:2], ss[:, 0:2], 1.0 / D, [ss], [rstd])
        for s_ in range(2):
            P.op("dve", lambda e, s_=s_: e.scalar_tensor_tensor(out=memn[:, s_, :], in0=mT[:, s_, :],
                                                                 scalar=rstd[:, s_:s_ + 1], in1=gmem[:],
                                                                 op0=ALU.mult, op1=ALU.mult), [mT, rstd, gmem], [memn])
        for c4 in range(2):
            tp = tp_p.next()
            for cc in range(4):
                c = 4 * c4 + cc
                for s_ in range(2):
                    P.op("pe", lambda e, s_=s_, c=c, cc=cc, tp=tp: e.transpose(
                        tp[:, cc * 256 + s_ * 128:cc * 256 + (s_ + 1) * 128], memn[:, s_, c * 128:(c + 1) * 128], ident[:]),
                        [memn, ident], [tp])
            P.op("act", lambda e, c4=c4, tp=tp: e.activation(out=memnT[:, 4 * c4:4 * c4 + 4, :],
                                                            in_=tp[:].rearrange("p (c t) -> p c t", c=4), func=AF.Copy),
                 [tp], [memnT])
        for c in range(8):
            pa = pa_p.next()
            for k in range(8):
                P.op("pe", lambda e, k=k, c=c, pa=pa: e.matmul(pa[:, 0:MEM], lhsT=wkv[:, k, c * 128:(c + 1) * 128],
                                                                rhs=memnT[:, k, :], start=(k == 0), stop=(k == 7)),
                     [wkv, memnT], [pa])
            P.op("act", lambda e, c=c, pa=pa: e.activation(out=ksqT[:, c, :], in_=pa[:, 0:MEM], func=AF.Square),
                 [pa], [ksqT, pa])
            P.op("dve", lambda e, c=c, pa=pa: e.tensor_copy(out=kT[:, c, :], in_=pa[:, 0:MEM]), [pa], [kT])
        for mb in range(2):
            for n in range(2):
                pa = pa_p.next()
                for k in range(8):
                    P.op("pe", lambda e, k=k, mb=mb, n=n, pa=pa: e.matmul(
                        pa[:], lhsT=memnT[:, k, mb * 128:(mb + 1) * 128], rhs=wkv[:, k, D + n * 512:D + (n + 1) * 512],
                        start=(k == 0), stop=(k == 7)), [wkv, memnT], [pa])
                P.op("act", lambda e, mb=mb, n=n, pa=pa: e.activation(out=v_tm[:, mb, n * 512:(n + 1) * 512], in_=pa[:],
                                                                       func=AF.Copy), [pa], [v_tm])
        for h in range(4):
            pd = pd_p.next()
            for cc in range(2):
                P.op("pe", lambda e, h=h, cc=cc, pd=pd: e.matmul(pd[:, 0:MEM], lhsT=ones_m[:], rhs=ksqT[:, 2 * h + cc, :],
                                                                  start=(cc == 0), stop=(cc == 1)), [ones_m, ksqT], [pd])
            P.op("dve", lambda e, h=h, pd=pd: e.reduce_max(out=kmax2[:, h:h + 1], in_=pd[:, 0:MEM],
                                                           axis=mybir.AxisListType.X), [pd], [kmax2])
        P.op("dve", lambda e: e.tensor_scalar(out=kmax2[:], in0=kmax2[:], scalar1=1.05 / 256.0, scalar2=None,
                                              op0=ALU.mult), [kmax2], [kmax2])
        P.barrier()
        P.release(m1)
        hT_p = P.pool("hT", [128, 4, D], F32, 2)
        hn = P.sbuf("hn", [128, 4, D], BF16)
        hnT_p = P.pool("hnT", [128, 8, TT], BF16, 2)
        qT = P.sbuf("qT", [128, 8, TT], BF16)
        qsqT = P.sbuf("qsqT", [128, 8, TT], BF16)
        mrow = P.sbuf("mrow", [128, 4, TT], BF16)
        lnt = P.sbuf("lnt", [128, TT], F32)
        pT_p = P.pool("pT", [128, 2, TT], BF16, 2)
        rden_p = P.pool("rden", [128, TT], F32, 2)
        oT = P.sbuf("oT", [128, 8, TT], BF16)
        ytmp = P.sbuf("ytmp", [128, D], F32)
        ss2 = P.sbuf("ss2", [128, 8], F32)
        rstd2 = P.sbuf("rstd2", [128, 4], F32)
        hTs = [None] * NT
        hnTs = [None] * NT
        hTs[0] = hT_p.next()
        hnTs[0] = hnT_p.next()
        self.front(src, 0, hTs[0], gpre, hn, hnTs[0], ss, rstd, tp_p, use_ln=True)
        for t in range(NT):
            hT, hnT = hTs[t], hnTs[t]
            for c in range(8):
                pa = pa_p.next()
                for k in range(8):
                    P.op("pe", lambda e, k=k, c=c, pa=pa, hnT=hnT: e.matmul(
                        pa[:], lhsT=wq[:, k, c * 128:(c + 1) * 128], rhs=hnT[:, k, :], start=(k == 0), stop=(k == 7)),
                        [wq, hnT], [pa])
                P.op("act", lambda e, c=c, pa=pa: e.activation(out=qsqT[:, c, :], in_=pa[:], func=AF.Square),
                     [pa], [qsqT, pa])
                P.op("dve", lambda e, c=c, pa=pa: e.tensor_scalar(out=qT[:, c, :], in0=pa[:], scalar1=1.0 / 16.0,
                                                                   scalar2=None, op0=ALU.mult), [pa], [qT])
            for h in range(4):
                pd = pd_p.next()
                for cc in range(2):
                    P.op("pe", lambda e, h=h, cc=cc, pd=pd: e.matmul(pd[:], lhsT=ones_m[:], rhs=qsqT[:, 2 * h + cc, :],
                                                                      start=(cc == 0), stop=(cc == 1)), [ones_m, qsqT], [pd])
                P.op("act", lambda e, h=h, pd=pd: e.activation(out=lnt[:], in_=pd[:], func=AF.Ln,
                                                               scale=kmax2[:, h:h + 1], bias=self.eps_t[:]),
                     [pd, kmax2, self.eps_t], [lnt])
                P.op("act", lambda e, h=h: e.activation(out=mrow[:, h, :], in_=lnt[:], func=AF.Exp, scale=0.5), [lnt], [mrow])
            def xa_scores(h):
                pT = pT_p.next()
                for mb in range(2):
                    pa = pa_p.next()
                    for cc in range(2):
                        P.op("pe", lambda e, h=h, mb=mb, cc=cc, pa=pa: e.matmul(
                            pa[:], lhsT=kT[:, 2 * h + cc, mb * 128:(mb + 1) * 128], rhs=qT[:, 2 * h + cc, :],
                            start=(cc == 0), stop=False), [kT, qT], [pa])
                    P.op("pe", lambda e, h=h, pa=pa: e.matmul(pa[:], lhsT=self.neg128[:], rhs=mrow[:, h, :],
                                                              start=False, stop=True), [self.neg128, mrow], [pa])
                    P.op("act", lambda e, mb=mb, pa=pa, pT=pT: e.activation(out=pT[:, mb, :], in_=pa[:], func=AF.Exp),
                         [pa], [pT])
                return pT

            def xa_pv(h, pT):
                pd = pd_p.next()
                for mb in range(2):
                    P.op("pe", lambda e, mb=mb, pd=pd, pT=pT: e.matmul(pd[:], lhsT=ones_m[:], rhs=pT[:, mb, :],
                                                                        start=(mb == 0), stop=(mb == 1)), [ones_m, pT], [pd])
                rden = rden_p.next()
                P.op("act", lambda e, pd=pd, rden=rden: e.activation(out=rden[:], in_=pd[:], func=AF.Ln), [pd], [rden])
                P.op("act", lambda e, rden=rden: e.activation(out=rden[:], in_=rden[:], func=AF.Exp, scale=-1.0), [rden], [rden])
                for ec in range(2):
                    po = po_p.next()
                    for mb in range(2):
                        P.op("pe", lambda e, h=h, ec=ec, mb=mb, po=po, pT=pT: e.matmul(
                            po[:], lhsT=v_tm[:, mb, h * 256 + ec * 128:h * 256 + (ec + 1) * 128], rhs=pT[:, mb, :],
                            start=(mb == 0), stop=(mb == 1)), [v_tm, pT], [po])
                    P.op("dve", lambda e, h=h, ec=ec, po=po, rden=rden: e.tensor_tensor(
                        out=oT[:, 2 * h + ec, :], in0=po[:], in1=rden[:], op=ALU.mult), [po, rden], [oT])

            pTs = xa_scores(0)
            for h in range(4):
                nxt_pT = xa_scores(h + 1) if h + 1 < 4 else None
                if h == 1 and t + 1 < NT:
                    hTs[t + 1] = hT_p.next()
                    hnTs[t + 1] = hnT_p.next()
                    self.front_a(src, t + 1, hTs[t + 1], gpre, hn, ss, rstd, use_ln=True)
                xa_pv(h, pTs)
                pTs = nxt_pT
            if t + 1 < NT:
                self.front_b(hn, hnTs[t + 1], tp_p)
            self.back(dst, t, hT, oT, 8, lambda k, n: wox[:, k, n * 512:(n + 1) * 512], [wox], gpost,
                      pa_p, ytmp, ss2, rstd2, use_ln=True)
        P.barrier()
        P.release(m0)

    def mixer(self, l, src, dst):
        self.mixer_a(l, src)
        self.mixer_b(l)
        self.mixer_c(l, src, dst)

    def mixer_a(self, l, src):
        P = self.P
        ident, junk, ones_m = self.ident, self.junk, self.ones_m
        eps_t = self.eps_t
        kp = self.kprep[(l, "mix")]
        m0 = P.mark()
        w_in = P.sbuf("w_in", [128, 8, NIN], BF16)
        P.dma(w_in[:], self.win_s[l].rearrange("(k p) n -> p k n", p=128), reads=[kp], writes=[w_in])
        gpre = P.sbuf("gpre", [128, D], F32)
        self.load_gain(gpre, self.norm_pre[l, 1])
        lng = P.sbuf("lng", [128, 256], F32)
        P.dma(lng[:], self.sgu_ln_g[l].partition_broadcast(128), writes=[lng])
        Bb = P.sbuf("Bb", [128, 512], F32)
        P.dma(Bb[:], self.sgu_b[l].rearrange("g t -> (g t)").partition_broadcast(128), writes=[Bb])
        ng = P.sbuf("ng", [128, 512], F32)
        P.dma(ng[:], self.ssm_norm_g[l].partition_broadcast(128), writes=[ng])
        dtb = P.sbuf("dtb", [128, 8], F32)
        P.dma(dtb[:], self.dt_bias[l].partition_broadcast(128), writes=[dtb])
        Aneg = P.sbuf("Aneg", [128, 8], F32)
        P.dma(Aneg[:], self.a_log[l].partition_broadcast(128), writes=[Aneg])
        P.op("act", lambda e: e.activation(out=Aneg[:], in_=Aneg[:], func=AF.Exp), [Aneg], [Aneg])
        P.op("dve", lambda e: e.tensor_scalar(out=Aneg[:], in0=Aneg[:], scalar1=-1.0, scalar2=None, op0=ALU.mult),
             [Aneg], [Aneg])
        Dd = P.sbuf("Dd", [128, 8], F32)
        P.dma(Dd[:], self.ssm_d[l].partition_broadcast(128), writes=[Dd])
        cw = P.sbuf("cw", [128, 4, 8], F32)
        for k in range(4):
            P.dma(cw[:, k, :], self.conv_w[l, k].rearrange("(c p) -> p c", p=128), writes=[cw],
                  allow_slow_non_contiguous=True)
        cb = P.sbuf("cb", [128, 8], F32)
        P.dma(cb[:], self.conv_b[l].rearrange("(c p) -> p c", p=128), writes=[cb], allow_slow_non_contiguous=True)
        U = P.sbuf("U", [128, 128], F32)
        Lm = P.sbuf("Lm", [128, 128], F32)
        ones_f = P.sbuf("ones_f", [128, 128], F32)
        P.dma(U[:], self.c_tri[0], writes=[U])
        P.dma(Lm[:], self.c_tri[1], writes=[Lm])
        P.dma(ones_f[:], self.c_tri[2], writes=[ones_f])
        pf = P.pool("pf", [128, TT], F32, 5, psum=True)
        pyo_p = P.pool("pyo", [128, TT], F32, 1, psum=True)
        pb = P.pool("pb", [128, 2 * TT], BF16, 2, psum=True)
        WsT = P.sbuf("WsT", [128, 4, 128], BF16)
        m1 = P.mark()
        ws_nat = P.sbuf("ws_nat", [128, 4, 128], F32)
        ws_bf = P.sbuf("ws_bf", [128, 4, 128], BF16)
        P.dma(ws_nat[:], self.sgu_w[l].rearrange("g t s -> t g s"), writes=[ws_nat])
        P.op("dve", lambda e: e.tensor_copy(out=ws_bf[:], in_=ws_nat[:]), [ws_nat], [ws_bf])
        tp = pb.next()
        for g in range(4):
            P.op("pe", lambda e, g=g, tp=tp: e.transpose(tp[:, g * 128:(g + 1) * 128], ws_bf[:, g, :], ident[:]),
                 [ws_bf, ident], [tp])
        P.op("dve", lambda e, tp=tp: e.tensor_tensor(out=WsT[:], in0=tp[:, 0:512].rearrange("p (g t) -> p g t", g=4),
                                                     in1=U[:].unsqueeze(1).to_broadcast([128, 4, 128]), op=ALU.mult),
             [tp, U], [WsT])
        P.barrier()
        P.release(m1)
        hT = P.sbuf("hT", [128, 4, D], F32)
        hn = P.sbuf("hn", [128, 4, D], BF16)
        hnT = P.sbuf("hnT", [128, 8, TT], BF16)
        ss = P.sbuf("ss", [128, 4], F32)
        rstd = P.sbuf("rstd", [128, 4], F32)
        uT = P.sbuf("uT", [64, 4, TT], BF16)
        vg = P.sbuf("vg", [128, 4, 256], F32)
        vtmp = P.sbuf("vtmp", [128, 256], F32)
        vn = P.sbuf("vn", [128, 4, 256], BF16)
        vst = P.sbuf("vst", [128, 24], F32)
        zs = P.sbuf("zs", [128, 4, TT], F32)
        xbcT = P.sbuf("xbcT", [128, 8, TT + 4], BF16)
        xact = P.sbuf("xact", [128, 8, TT], BF16)
        dg = P.sbuf("dg", [128, 32, 128], BF16)
        for c in range(8):
            for k in range(4):
                P.op("dve", lambda e, c=c, k=k: e.tensor_scalar(out=dg[:, c * 4 + k, :], in0=self.ident_f[:],
                                                                 scalar1=cw[:, k, c:c + 1], scalar2=None, op0=ALU.mult),
                     [self.ident_f, cw], [dg])
        dts = P.sbuf("dts", [128, 6, 32], F32)
        a_t = P.sbuf("a_t", [128, 4, 8], F32)
        qkvst = P.sbuf("qkvst", [128, 6, TT], BF16)
        aoutT = P.sbuf("aoutT", [64, 4, TT], BF16)
        boutT = P.sbuf("boutT", [128, 4, TT], BF16)
        sgt = P.sbuf("sgt", [64, 512], F32)
        B_tm = P.sbuf("B_tm", [128, 256], BF16)
        x_tm = P.sbuf("x_tm", [128, 512], BF16)
        xsD_p = P.pool("xsD", [128, 512], F32, 1)
        ypre_p = P.pool("ypre", [128, 512], F32, 2)
        stS_p = P.pool("stS", [128, 512], F32, 2)
        cbm = P.sbuf("cbm", [128, 2, 128], F32)
        R = P.sbuf("R", [128, 8, 128], F32)
        decT = P.sbuf("decT", [128, 8, 128], F32)
        eacs_p = P.pool("eacs", [128, 16], F32, 2)
        MT = P.sbuf("MT", [128, 8, 128], BF16)
        xd = P.sbuf("xd", [128, 512], BF16)
        y = P.sbuf("y", [128, 512], F32)
        ynb = P.sbuf("ynb", [128, 512], BF16)
        hst = P.sbuf("hst", [128, 512], F32)
        hst_bf = P.sbuf("hst_bf", [128, 512], BF16)
        ssg = P.sbuf("ssg", [128, 4], F32)
        P.op("dve", lambda e: e.memset(hst[:], 0.0), [], [hst])
        P.op("dve", lambda e: e.memset(hst_bf[:], 0.0), [], [hst_bf])
        P.op("dve", lambda e: e.memset(xbcT[:, :, 0:3], 0.0), [], [xbcT])

        def proj_fm(pbuf, M, col0):
            for k in range(8):
                P.op("pe", lambda e, k=k: e.matmul(pbuf[0:M, :], lhsT=w_in[:, k, col0:col0 + M], rhs=hnT[:, k, :],
                                                   start=(k == 0), stop=(k == 7)), [w_in, hnT], [pbuf])

        def proj_tm(pview, pbuf, s_, col0, N):
            for k in range(8):
                P.op("pe", lambda e, k=k: e.matmul(pview, lhsT=hnT[:, k, s_ * 128:(s_ + 1) * 128],
                                                   rhs=w_in[:, k, col0:col0 + N], start=(k == 0), stop=(k == 7)),
                     [w_in, hnT], [pbuf])

        for t in range(NT):
            self.front(src, t, hT, gpre, hn, hnT, ss, rstd, pb, use_ln=True)
            for g in range(4):
                pu = pf.next()
                proj_fm(pu, 64, g * 64)
                P.op("act", lambda e, g=g, pu=pu: e.activation(out=uT[0:64, g, :], in_=pu[0:64, :],
                                                               func=AF.Gelu_apprx_tanh), [pu], [uT])
            for s2 in range(2):
                pv = pf.next()
                for s1 in range(2):
                    s_ = 2 * s2 + s1
                    proj_tm(pv[:, s1 * 256:(s1 + 1) * 256], pv, s_, 256, 256)
                for s1 in range(2):
                    s_ = 2 * s2 + s1
                    P.op("act", lambda e, s_=s_, s1=s1, pv=pv: e.activation(
                        out=vg[:, s_, :], in_=pv[:, s1 * 256:(s1 + 1) * 256], func=AF.Gelu_apprx_tanh,
                        accum_out=vst[:, s_:s_ + 1]), [pv], [vg, vst])
                    P.op("act", lambda e, s_=s_: e.activation(out=junk[:, 0:256], in_=vg[:, s_, :], func=AF.Square,
                                                              accum_out=vst[:, 4 + s_:5 + s_]), [vg], [junk, vst])
            P.op("dve", lambda e: e.tensor_scalar(out=vst[:, 8:12], in0=vst[:, 0:4], scalar1=1.0 / 256, scalar2=None,
                                                  op0=ALU.mult), [vst], [vst])
            P.op("dve", lambda e: e.tensor_tensor(out=vst[:, 12:16], in0=vst[:, 8:12], in1=vst[:, 8:12], op=ALU.mult),
                 [vst], [vst])
            P.op("dve", lambda e: e.scalar_tensor_tensor(out=vst[:, 16:20], in0=vst[:, 4:8], scalar=1.0 / 256,
                                                         in1=vst[:, 12:16], op0=ALU.mult, op1=ALU.subtract), [vst], [vst])
            self.rsqrt_lnexp(vst[:, 20:24], vst[:, 16:20], 1.0, [vst], [vst])
            for s_ in range(4):
                P.op("dve", lambda e, s_=s_: e.tensor_scalar(out=vtmp[:], in0=vg[:, s_, :], scalar1=vst[:, 8 + s_:9 + s_],
                                                             scalar2=vst[:, 20 + s_:21 + s_], op0=ALU.subtract,
                                                             op1=ALU.mult), [vg, vst], [vtmp])
                P.op("dve", lambda e, s_=s_: e.tensor_tensor(out=vn[:, s_, :], in0=vtmp[:], in1=lng[:], op=ALU.mult),
                     [vtmp, lng], [vn])
            for s_ in range(4):
                pz = pf.next()
                proj_tm(pz[:], pz, s_, 512, 512)
                P.op("act", lambda e, s_=s_, pz=pz: e.activation(out=zs[:, s_, :], in_=pz[:], func=AF.Silu), [pz], [zs])
            for c in range(8):
                px = pf.next()
                proj_fm(px, 128, 1024 + c * 128)
                if c % 2 == 0:
                    P.op("act", lambda e, c=c, px=px: e.activation(out=xbcT[:, c, 3:3 + TT], in_=px[:], func=AF.Copy),
                         [px], [xbcT])
                else:
                    P.op("dve", lambda e, c=c, px=px: e.tensor_copy(out=xbcT[:, c, 3:3 + TT], in_=px[:]), [px], [xbcT])
            for c in range(8):
                pc = pf.next()
                for k in range(4):
                    P.op("pe", lambda e, c=c, k=k, pc=pc: e.matmul(pc[:], lhsT=dg[:, c * 4 + k, :], rhs=xbcT[:, c, k:k + TT],
                                                                    start=(k == 0), stop=(k == 3)), [dg, xbcT], [pc])
                P.op("act", lambda e, c=c, pc=pc: e.activation(out=xact[:, c, :], in_=pc[:], func=AF.Silu,
                                                               bias=cb[:, c:c + 1]), [pc, cb], [xact])
            P.op("dve", lambda e: e.tensor_copy(out=xbcT[:, :, 0:3], in_=xbcT[:, :, TT:TT + 3]), [xbcT], [xbcT])
            pdt = pf.next()
            for s_ in range(4):
                proj_tm(pdt[:, s_ * 8:(s_ + 1) * 8], pdt, s_, 2048, 8)
            P.op("dve", lambda e, pdt=pdt: e.tensor_tensor(out=dts[:, 0, :].rearrange("p (s h) -> p s h", s=4),
                                                           in0=pdt[:, 0:32].rearrange("p (s h) -> p s h", s=4),
                                                           in1=dtb[:].unsqueeze(1).to_broadcast([128, 4, 8]), op=ALU.add),
                 [pdt, dtb], [dts])
            P.op("act", lambda e: e.activation(out=dts[:, 1, :], in_=dts[:, 0, :], func=AF.Exp), [dts], [dts])
            P.op("act", lambda e: e.activation(out=dts[:, 2, :], in_=dts[:, 1, :], func=AF.Ln, bias=1.0), [dts], [dts])
            P.op("dve", lambda e: e.tensor_scalar(out=dts[:, 3, :], in0=dts[:, 1, :], scalar1=1.0 / 3.0, scalar2=-0.5,
                                                  op0=ALU.mult, op1=ALU.add), [dts], [dts])
            P.op("dve", lambda e: e.tensor_tensor(out=dts[:, 3, :], in0=dts[:, 3, :], in1=dts[:, 1, :], op=ALU.mult),
                 [dts], [dts])
            P.op("dve", lambda e: e.scalar_tensor_tensor(out=dts[:, 3, :], in0=dts[:, 3, :], scalar=1.0, in1=dts[:, 1, :],
                                                         op0=ALU.add, op1=ALU.mult), [dts], [dts])
            P.op("dve", lambda e: e.tensor_single_scalar(out=dts[:, 4, :], in_=dts[:, 1, :], scalar=0.01, op=ALU.is_lt),
                 [dts], [dts])
            P.op("dve", lambda e: e.tensor_tensor(out=dts[:, 3, :], in0=dts[:, 3, :], in1=dts[:, 2, :], op=ALU.subtract),
                 [dts], [dts])
            P.op("dve", lambda e: e.tensor_tensor(out=dts[:, 3, :], in0=dts[:, 3, :], in1=dts[:, 4, :], op=ALU.mult),
                 [dts], [dts])
            P.op("dve", lambda e: e.tensor_tensor(out=dts[:, 5, :], in0=dts[:, 3, :], in1=dts[:, 2, :], op=ALU.add),
                 [dts], [dts])
            P.op("dve", lambda e: e.tensor_tensor(out=a_t[:], in0=dts[:, 5, :].rearrange("p (s h) -> p s h", s=4),
                                                  in1=Aneg[:].unsqueeze(1).to_broadcast([128, 4, 8]), op=ALU.mult),
                 [dts, Aneg], [a_t])
            def qkv_proj(j):
                pq = pf.next()
                proj_fm(pq, 128, 2056 + j * 128)
                P.op("act", lambda e: e.activation(out=qkvst[:, j, :], in_=pq[:], func=AF.Copy,
                                                   scale=(0.125 if j < 2 else 1.0)), [pq], [qkvst])

            def stage_a(s_):
                l0 = s_ * 128
                xsD, eacs = xsD_p.next(), eacs_p.next()
                psg = pf.next()
                for g in range(4):
                    P.op("pe", lambda e, g=g: e.matmul(
                        psg[0:64, g * 128:(g + 1) * 128], lhsT=vn[:, s_, g * 64:(g + 1) * 64], rhs=WsT[:, g, :],
                        start=True, stop=True), [vn, WsT], [psg])
                P.op("dve", lambda e: e.tensor_tensor(out=sgt[0:64, :], in0=psg[0:64, :], in1=Bb[0:64, :], op=ALU.add),
                     [psg, Bb], [sgt])
                P.op("dve", lambda e: e.tensor_tensor(
                    out=aoutT[0:64, :, l0:l0 + 128], in0=sgt[0:64, :].rearrange("p (g t) -> p g t", g=4),
                    in1=uT[0:64, :, l0:l0 + 128], op=ALU.mult), [sgt, uT], [aoutT])
                tpx = pb.next()
                for c in range(4):
                    P.op("pe", lambda e, c=c: e.transpose(tpx[:, c * 128:(c + 1) * 128], xact[:, c, l0:l0 + 128], ident[:]),
                         [xact, ident], [tpx])
                for g in range(2):
                    P.op("pe", lambda e, g=g: e.transpose(tpx[:, 512 + g * 128:512 + (g + 1) * 128],
                                                          xact[:, 4 + g, l0:l0 + 128], ident[:]), [xact, ident], [tpx])
                P.op("act", lambda e: e.activation(out=B_tm[:], in_=tpx[:, 512:768], func=AF.Copy), [tpx], [B_tm, tpx])
                P.op("dve", lambda e: e.tensor_tensor(
                    out=x_tm[:].rearrange("p (h q) -> p h q", h=8), in0=tpx[:, 0:512].rearrange("p (h q) -> p h q", h=8),
                    in1=dts[:, 5, s_ * 8:(s_ + 1) * 8].unsqueeze(2).to_broadcast([128, 8, 64]), op=ALU.mult),
                    [tpx, dts], [x_tm])
                P.op("dve", lambda e: e.tensor_tensor(
                    out=xsD[:].rearrange("p (h q) -> p h q", h=8), in0=tpx[:, 0:512].rearrange("p (h q) -> p h q", h=8),
                    in1=Dd[:].unsqueeze(2).to_broadcast([128, 8, 64]), op=ALU.mult), [tpx, Dd], [xsD])
                pcb = pf.next()
                for g in range(2):
                    P.op("pe", lambda e, g=g: e.matmul(
                        pcb[:, g * 128:(g + 1) * 128], lhsT=xact[:, 4 + g, l0:l0 + 128], rhs=xact[:, 6 + g, l0:l0 + 128],
                        start=True, stop=True), [xact], [pcb])
                P.op("dve", lambda e: e.tensor_tensor(
                    out=cbm[:], in0=pcb[:, 0:256].rearrange("p (g l) -> p g l", g=2),
                    in1=U[:].unsqueeze(1).to_broadcast([128, 2, 128]), op=ALU.mult), [pcb, U], [cbm])
                P.op("dve", lambda e: e.tensor_tensor(
                    out=R[:], in0=a_t[:, s_, :].unsqueeze(2).to_broadcast([128, 8, 128]),
                    in1=U[:].unsqueeze(1).to_broadcast([128, 8, 128]), op=ALU.mult), [a_t, U], [R])
                psg2 = [pf.next(), pf.next()]
                for b_ in range(2):
                    P.op("pe", lambda e, b_=b_: e.matmul(
                        psg2[b_][:], lhsT=Lm[:], rhs=R[:, 4 * b_:4 * b_ + 4, :].rearrange("p h l -> p (h l)"),
                        start=True, stop=True), [Lm, R], [psg2[b_]])
                pac = pf.next()
                P.op("pe", lambda e: e.matmul(pac[:, 0:8], lhsT=U[:], rhs=a_t[:, s_, :], start=True, stop=True),
                     [U, a_t], [pac])
                P.op("pe", lambda e: e.matmul(pac[:, 8:16], lhsT=ones_f[:], rhs=a_t[:, s_, :], start=True, stop=True),
                     [ones_f, a_t], [pac])
                for b_ in range(2):
                    P.op("act", lambda e, b_=b_: e.activation(
                        out=decT[:, 4 * b_:4 * b_ + 4, :].rearrange("p h l -> p (h l)"), in_=psg2[b_][:], func=AF.Exp),
                        [psg2[b_]], [decT])
                P.op("act", lambda e: e.activation(out=eacs[:], in_=pac[:, 0:16], func=AF.Exp), [pac], [eacs])
                return dict(l0=l0, s_=s_, eacs=eacs, xsD=xsD)

            def stage_a2(st):
                l0, s_, eacs, xsD = st["l0"], st["s_"], st["eacs"], st["xsD"]
                P.op("dve", lambda e: e.tensor_tensor(
                    out=MT[:].rearrange("p (g e) l -> p g e l", g=2), in0=decT[:].rearrange("p (g e) l -> p g e l", g=2),
                    in1=cbm[:].unsqueeze(2).to_broadcast([128, 2, 4, 128]), op=ALU.mult), [decT, cbm], [MT])
                P.op("dve", lambda e: e.tensor_tensor(
                    out=xd[:].rearrange("p (h q) -> p h q", h=8), in0=x_tm[:].rearrange("p (h q) -> p h q", h=8),
                    in1=decT[:, :, 127:128].to_broadcast([128, 8, 64]), op=ALU.mult), [x_tm, decT], [xd])
                pyd = pf.next()
                for h in range(8):
                    P.op("pe", lambda e, h=h: e.matmul(pyd[:, h * 64:(h + 1) * 64], lhsT=MT[:, h, :],
                                                       rhs=x_tm[:, h * 64:(h + 1) * 64], start=True, stop=True),
                         [MT, x_tm], [pyd])
                pst = pf.next()
                for g in range(2):
                    P.op("pe", lambda e, g=g: e.matmul(
                        pst[:, g * 256:(g + 1) * 256], lhsT=B_tm[:, g * 128:(g + 1) * 128], rhs=xd[:, g * 256:(g + 1) * 256],
                        start=True, stop=True), [B_tm, xd], [pst])
                ypre, stS = ypre_p.next(), stS_p.next()
                P.op("dve", lambda e: e.tensor_tensor(out=ypre[:], in0=pyd[:], in1=xsD[:], op=ALU.add), [pyd, xsD], [ypre])
                P.op("act", lambda e: e.activation(out=stS[:], in_=pst[:], func=AF.Copy), [pst], [stS])
                st["ypre"] = ypre
                st["stS"] = stS
                return st

            def stage_b0(st):
                l0 = st["l0"]
                pyo = pyo_p.next()
                for g in range(2):
                    P.op("pe", lambda e, g=g: e.matmul(
                        pyo[:, g * 256:(g + 1) * 256], lhsT=xact[:, 6 + g, l0:l0 + 128], rhs=hst_bf[:, g * 256:(g + 1) * 256],
                        start=True, stop=True), [xact, hst_bf], [pyo])
                st["pyo"] = pyo

            def stage_b(st):
                l0, s_, eacs, ypre, stS, pyo = st["l0"], st["s_"], st["eacs"], st["ypre"], st["stS"], st["pyo"]
                P.op("dve", lambda e: e.tensor_tensor(
                    out=y[:].rearrange("p (h q) -> p h q", h=8), in0=pyo[:].rearrange("p (h q) -> p h q", h=8),
                    in1=eacs[:, 0:8].unsqueeze(2).to_broadcast([128, 8, 64]), op=ALU.mult), [pyo, eacs], [y])
                P.op("dve", lambda e: e.tensor_tensor(out=y[:], in0=y[:], in1=ypre[:], op=ALU.add), [y, ypre], [y])
                P.op("dve", lambda e: e.tensor_tensor(
                    out=hst[:].rearrange("p (h q) -> p h q", h=8), in0=hst[:].rearrange("p (h q) -> p h q", h=8),
                    in1=eacs[:, 8:16].unsqueeze(2).to_broadcast([128, 8, 64]), op=ALU.mult), [hst, eacs], [hst])
                P.op("dve", lambda e: e.tensor_tensor(out=hst[:], in0=hst[:], in1=stS[:], op=ALU.add), [hst, stS], [hst])
                P.op("act", lambda e: e.activation(out=hst_bf[:], in_=hst[:], func=AF.Copy), [hst], [hst_bf])
                P.op("dve", lambda e: e.tensor_tensor(out=y[:], in0=y[:], in1=zs[:, s_, :], op=ALU.mult), [y, zs], [y])
                for g in range(2):
                    P.op("act", lambda e, g=g: e.activation(out=junk[:, 0:256], in_=y[:, g * 256:(g + 1) * 256],
                                                            func=AF.Square, accum_out=ssg[:, g:g + 1]), [y], [junk, ssg])
                self.rsqrt_lnexp(ssg[:, 2:4], ssg[:, 0:2], 1.0 / 256, [ssg], [ssg])
                P.op("dve", lambda e: e.tensor_tensor(
                    out=y[:].rearrange("p (g q) -> p g q", g=2), in0=y[:].rearrange("p (g q) -> p g q", g=2),
                    in1=ssg[:, 2:4].unsqueeze(2).to_broadcast([128, 2, 256]), op=ALU.mult), [y, ssg], [y])
                P.op("dve", lambda e: e.tensor_tensor(out=ynb[:], in0=y[:], in1=ng[:], op=ALU.mult), [y, ng], [ynb])
                tpy = pb.next()
                for c in range(4):
                    P.op("pe", lambda e, c=c: e.transpose(tpy[:, c * 128:(c + 1) * 128], ynb[:, c * 128:(c + 1) * 128], ident[:]),
                         [ynb, ident], [tpy])
                P.op("act", lambda e: e.activation(
                    out=boutT[:, :, l0:l0 + 128], in_=tpy[:, 0:512].rearrange("p (c t) -> p c t", c=4), func=AF.Copy),
                    [tpy], [boutT])

            st_cur = stage_a2(stage_a(0))
            for s_ in range(4):
                stage_b0(st_cur)
                st_nxt = stage_a(s_ + 1) if s_ + 1 < 4 else None
                qkv_proj(s_)
                stage_b(st_cur)
                if st_nxt is not None:
                    st_nxt = stage_a2(st_nxt)
                if s_ + 4 < 6:
                    qkv_proj(s_ + 4)
                st_cur = st_nxt
            P.dma(self.qkv_s[:, :, t * TT:(t + 1) * TT].rearrange("j p t -> p j t"), qkvst[:], reads=[qkvst],
                  writes=[self.k_qkv])
            P.dma(self.a_s[:, :, t * TT:(t + 1) * TT].rearrange("g p t -> p g t"), aoutT[0:64, :, :], reads=[aoutT],
                  writes=[self.k_ab])
            P.dma(self.b_s[:, :, t * TT:(t + 1) * TT].rearrange("g p t -> p g t"), boutT[:], reads=[boutT],
                  writes=[self.k_ab])
        P.barrier()
        P.release(m0)

    def mixer_b(self, l):
        P = self.P
        ident, ident_f, junk, ones_m, negones, biasT = self.ident, self.ident_f, self.junk, self.ones_m, self.negones, self.biasT
        m0 = P.mark()
        qT = P.sbuf("qT", [128, 2, S], BF16)
        kT = P.sbuf("kT", [128, 2, S], BF16)
        vT = P.sbuf("vT", [128, 2, S], BF16)
        P.dma(qT[:], self.qkv_s[0:2].rearrange("j p t -> p j t"), reads=[self.k_qkv], writes=[qT])
        P.dma(kT[:], self.qkv_s[2:4].rearrange("j p t -> p j t"), reads=[self.k_qkv], writes=[kT])
        P.dma(vT[:], self.qkv_s[4:6].rearrange("j p t -> p j t"), reads=[self.k_qkv], writes=[vT])
        acc = P.sbuf("acc", [128, S], F32)
        cst = P.sbuf("cst", [64, S], BF16)
        rden_p = P.pool("rden", [64, TT], F32, 2)
        sq = P.sbuf("sq", [128, S], BF16)
        kh = P.sbuf("kh", [128, S], BF16)
        qh = P.sbuf("qh", [128, S], BF16)
        kmx = P.sbuf("kmx", [128, 16], F32)
        lnr = P.sbuf("lnr", [128, TT], F32)
        v_tm = [P.sbuf("v_tm%d" % p, [128, 32, 2, 128], BF16) for p in range(3)]
        for p in range(3):
            P.op("dve", lambda e, p=p: e.memset(v_tm[p][:, :, :, 64:128], 1.0), [], [v_tm[p]])
        pT_p = P.pool("pT", [128, 256], BF16, 3)
        ps_p = P.pool("ps", [128, TT], F32, 3, psum=True)
        pq_p = P.pool("pq", [128, TT], F32, 3, psum=True)
        pb = P.pool("pb", [128, 2 * TT], BF16, 2, psum=True)
        dils = (1, 4, 16)
        for c in range(2):
            for p, dil in enumerate(dils):
                nb = 32 // dil
                vv = vT[:, c, :].rearrange("p (l r) -> p r l", r=dil)
                for b8 in range(4):
                    tp = pb.next()
                    for bb in range(8):
                        b = b8 * 8 + bb
                        r, n = b // nb, b % nb
                        P.op("pe", lambda e, bb=bb, r=r, n=n, vv=vv, tp=tp: e.transpose(
                            tp[:, bb * 128:(bb + 1) * 128], vv[:, r, n * 128:(n + 1) * 128], ident[:]), [vT, ident], [tp])
                    dstv = v_tm[p][:, b8 * 8:(b8 + 1) * 8, :, 0:64]
                    srcv = tp[:].rearrange("p (b h e) -> p b h e", b=8, h=2)
                    if b8 % 2 == 0:
                        P.op("act", lambda e, dstv=dstv, srcv=srcv: e.activation(out=dstv, in_=srcv, func=AF.Copy),
                             [tp], [v_tm[p]])
                    else:
                        P.op("dve", lambda e, dstv=dstv, srcv=srcv: e.tensor_copy(out=dstv, in_=srcv), [tp], [v_tm[p]])
            for hh in range(2):
                h = 2 * c + hh
                p0 = hh * 64
                q0 = 64 - p0
                P.op("act", lambda e, c=c: e.activation(out=sq[:], in_=kT[:, c, :], func=AF.Square), [kT], [sq])
                for j in range(8):
                    pm = ps_p.next()
                    P.op("pe", lambda e, j=j, pm=pm, p0=p0: e.matmul(pm[:, :], lhsT=ones_m[p0:p0 + 64, :],
                                                                      rhs=sq[p0:p0 + 64, j * TT:(j + 1) * TT], start=True, stop=True),
                         [ones_m, sq], [pm])
                    P.op("dve", lambda e, j=j, pm=pm: e.reduce_max(out=kmx[:, j:j + 1], in_=pm[:, :],
                                                                   axis=mybir.AxisListType.X), [pm], [kmx])
                P.op("dve", lambda e: e.reduce_max(out=kmx[:, 8:9], in_=kmx[:, 0:8], axis=mybir.AxisListType.X),
                     [kmx], [kmx])
                P.op("dve", lambda e: e.tensor_scalar(out=kmx[:, 8:9], in0=kmx[:, 8:9], scalar1=1.1, scalar2=None,
                                                      op0=ALU.mult), [kmx], [kmx])
                P.op("dve", lambda e, p0=p0, c=c: e.tensor_copy(out=kh[p0:p0 + 64, :], in_=kT[p0:p0 + 64, c, :]), [kT], [kh])
                P.op("dve", lambda e, q0=q0: e.memset(kh[q0:q0 + 64, :], -1.0 / 64.0), [], [kh])
                P.op("act", lambda e, p0=p0, c=c: e.activation(out=qh[p0:p0 + 64, :], in_=qT[p0:p0 + 64, c, :], func=AF.Copy),
                     [qT], [qh])
                P.op("act", lambda e, c=c: e.activation(out=sq[:], in_=qT[:, c, :], func=AF.Square), [qT], [sq])
                for j in range(8):
                    pm = ps_p.next()
                    P.op("pe", lambda e, j=j, pm=pm, p0=p0: e.matmul(pm[:, :], lhsT=ones_m[p0:p0 + 64, :],
                                                                      rhs=sq[p0:p0 + 64, j * TT:(j + 1) * TT], start=True, stop=True),
                         [ones_m, sq], [pm])
                    P.op("act", lambda e, j=j, pm=pm, q0=q0: e.activation(
                        out=lnr[q0:q0 + 64, :], in_=pm[q0:q0 + 64, :], func=AF.Ln, scale=kmx[q0:q0 + 64, 8:9],
                        bias=self.eps_t[q0:q0 + 64, :]), [pm, kmx, self.eps_t], [lnr])
                    P.op("act", lambda e, j=j, q0=q0: e.activation(out=qh[q0:q0 + 64, j * TT:(j + 1) * TT],
                                                                    in_=lnr[q0:q0 + 64, :], func=AF.Exp, scale=0.5), [lnr], [qh])
                for p, dil in enumerate(dils):
                    nb = 32 // dil
                    qv = qh[:, :].rearrange("p (l r) -> p r l", r=dil)
                    kv = kh[:, :].rearrange("p (l r) -> p r l", r=dil)
                    av = acc[:, :].rearrange("p (l r) -> p r l", r=dil)
                    blocks = [(r, n) for r in range(dil) for n in range(nb)]

                    def mb_scores(r, n, p=p, nb=nb, kv=kv, qv=qv, h=h):
                        W = 256 if n < nb - 1 else 128
                        ps_ = ps_p.next()
                        P.op("pe", lambda e: e.matmul(
                            ps_[:, 0:W], lhsT=kv[:, r, n * 128:(n + 1) * 128], rhs=qv[:, r, n * 128:n * 128 + W],
                            start=True, stop=False), [kh, qh], [ps_])
                        P.op("pe", lambda e: e.matmul(
                            ps_[:, 0:W], lhsT=ident[:], rhs=biasT[:, p * 4 + h, 0:W], start=False, stop=True),
                            [ident, biasT], [ps_])
                        pT = pT_p.next()
                        P.op("act", lambda e: e.activation(out=pT[:, 0:W], in_=ps_[:, 0:W], func=AF.Exp), [ps_], [pT])
                        return pT

                    state = {"pq_next": None}

                    def mb_pv(r, n, pT, p=p, nb=nb, av=av, hh=hh):
                        b = r * nb + n
                        pq_cur = state["pq_next"] if n > 0 else pq_p.next()
                        P.op("pe", lambda e: e.matmul(pq_cur[:, 0:128], lhsT=v_tm[p][:, b, hh, :], rhs=pT[:, 0:128],
                                                      start=(n == 0), stop=True), [v_tm[p], pT], [pq_cur])
                        dst = av[:, r, n * 128:(n + 1) * 128]
                        if p == 0:
                            P.op("dve", lambda e: e.tensor_copy(out=dst, in_=pq_cur[:, 0:128]), [pq_cur], [acc])
                        else:
                            P.op("dve", lambda e: e.tensor_tensor(out=dst, in0=dst, in1=pq_cur[:, 0:128], op=ALU.add),
                                 [pq_cur, acc], [acc])
                        if n < nb - 1:
                            pq_next = pq_p.next()
                            P.op("pe", lambda e: e.matmul(pq_next[:, 0:128], lhsT=v_tm[p][:, b, hh, :], rhs=pT[:, 128:256],
                                                          start=True, stop=False), [v_tm[p], pT], [pq_next])
                            state["pq_next"] = pq_next

                    pT_cur = mb_scores(*blocks[0])
                    for i, (r, n) in enumerate(blocks):
                        pT_nxt = mb_scores(*blocks[i + 1]) if i + 1 < len(blocks) else None
                        mb_pv(r, n, pT_cur)
                        pT_cur = pT_nxt
                for j in range(8):
                    pm = ps_p.next()
                    P.op("pe", lambda e, j=j, pm=pm: e.matmul(pm[0:64, :], lhsT=ident_f[:, 64:128], rhs=acc[:, j * TT:(j + 1) * TT],
                                                              start=True, stop=True), [ident_f, acc], [pm])
                    rden = rden_p.next()
                    P.op("act", lambda e, pm=pm, rden=rden: e.activation(out=rden[0:64, :], in_=pm[0:64, :], func=AF.Ln),
                         [pm], [rden])
                    P.op("act", lambda e, rden=rden: e.activation(out=rden[0:64, :], in_=rden[0:64, :], func=AF.Exp, scale=-1.0),
                         [rden], [rden])
                    P.op("dve", lambda e, j=j, rden=rden: e.tensor_tensor(out=cst[0:64, j * TT:(j + 1) * TT],
                                                                          in0=acc[0:64, j * TT:(j + 1) * TT], in1=rden[0:64, :],
                                                                          op=ALU.mult), [acc, rden], [cst])
                P.dma(self.c_s[h], cst[0:64, :], reads=[cst], writes=[self.k_c])
        P.barrier()
        P.release(m0)

    def mixer_c(self, l, src, dst):
        P = self.P
        kp = self.kprep[(l, "mix")]
        m0 = P.mark()
        w_o = P.sbuf("w_o", [128, 8, D], BF16)
        P.dma(w_o[:], self.wout_s[l].rearrange("(k p) n -> p k n", p=128), reads=[kp], writes=[w_o])
        gpost = P.sbuf("gpost", [128, D], F32)
        self.load_gain(gpost, self.norm_post[l, 1])
        hT_p = P.pool("hT", [128, 4, D], F32, 2)
        M_p = P.pool("Mst", [128, 8, TT], BF16, 2)
        ytmp = P.sbuf("ytmp", [128, D], F32)
        ss2 = P.sbuf("ss2", [128, 8], F32)
        rstd2 = P.sbuf("rstd2", [128, 4], F32)
        py_p = P.pool("py", [128, TT], F32, 4, psum=True)

        def loads(t):
            hT, M = hT_p.next(), M_p.next()
            P.dma(hT[:], src[t * TT:(t + 1) * TT, :].rearrange("(s p) d -> p s d", p=128), reads=[self.k_h], writes=[hT])
            P.dma(M[:, 0:2, :], self.a_s[:, :, t * TT:(t + 1) * TT].rearrange("(j g) p t -> (g p) j t", g=2),
                  reads=[self.k_ab], writes=[M])
            P.dma(M[:, 2:6, :], self.b_s[:, :, t * TT:(t + 1) * TT].rearrange("g p t -> p g t"), reads=[self.k_ab], writes=[M])
            P.dma(M[:, 6:8, :], self.c_s[:, :, t * TT:(t + 1) * TT].rearrange("(j g) p t -> (g p) j t", g=2),
                  reads=[self.k_c], writes=[M])
            return hT, M

        nxt = loads(0)
        for t in range(NT):
            hT, M = nxt
            if t + 1 < NT:
                nxt = loads(t + 1)
            self.back(dst, t, hT, M, 8, lambda k, n: w_o[:, k, n * 512:(n + 1) * 512], [w_o], gpost, py_p, ytmp, ss2, rstd2)
        P.barrier()
        P.release(m0)

    def copy_phase(self, src, dst):
        P = self.P
        m = P.mark()
        hT_p = P.pool("hT", [128, 4, D], F32, 2)
        for t in range(NT):
            hT = hT_p.next()
            P.dma(hT[:], src[t * TT:(t + 1) * TT, :].rearrange("(s p) d -> p s d", p=128), reads=[self.k_h], writes=[hT])
            kd = self.k_out if dst is self.out else self.k_h
            P.dma(dst[t * TT:(t + 1) * TT, :].rearrange("(s p) d -> p s d", p=128), hT[:], reads=[hT], writes=[kd])
        P.barrier()
        P.release(m)

    def build(self):
        P = self.P
        self.declare()
        plan = self.plan
        if plan is None:
            plan = [(l, s) for l in range(self.nl) for s in range(4)]
        layers = sorted(set(l for l, _ in plan))
        for l in layers:
            subs = set(s for ll, s in plan if ll == l)
            which = []
            if 0 in subs or 3 in subs or 10 in subs:
                which.append("ffn")
            if 1 in subs:
                which.append("mix")
            if 2 in subs:
                which.append("xa")
            self.prep_weights(l, which)
        self.consts()
        cur = self.x
        for i, (l, s) in enumerate(plan):
            dst = self.out if i == len(plan) - 1 else self.h
            if s >= 9:
                self.copy_phase(cur, dst)
            elif s == 0:
                self.ffn(l, 0, cur, dst)
            elif s == 3:
                self.ffn(l, 1, cur, dst)
            elif s == 1:
                self.mixer(l, cur, dst)
            elif s == 2:
                self.xattn(l, cur, dst)
            cur = dst
        P.op("sp", lambda e: e.nop(), reads=[self.k_out], writes=[])
        P.finish()
        return self.nc


def _t5_bucket_np(dist):
    dist = np.asarray(dist)
    max_exact = 16
    d = np.maximum(dist, 1).astype(np.float32)
    large = max_exact + (np.log(d / np.float32(max_exact)) / np.float32(math.log(2048 / max_exact))
                         * np.float32(32 - max_exact)).astype(np.int32)
    large = np.minimum(large, 31)
    return np.where(dist < max_exact, dist, large)


def host_consts(rel_bias):
    j = np.arange(128)[:, None]
    i = np.arange(128)[None, :]
    relb = np.zeros((3, 4, 128, 256), np.float32)
    amask = np.zeros((128, 256), np.float32)
    for p, dil in enumerate((1, 4, 16)):
        d_cur = np.maximum(i - j, 0)
        d_prev = np.clip(i - j + 128, 0, 128)
        b_cur = _t5_bucket_np(d_cur * dil)
        b_prev = _t5_bucket_np(d_prev * dil)
        for hh in range(4):
            relb[p, hh, :, 0:128] = rel_bias[b_cur, hh]
            relb[p, hh, :, 128:256] = rel_bias[b_prev, hh]
    amask[:, 0:128] = np.where(i >= j, 0.0, -30000.0)
    amask[:, 128:256] = np.where(i <= j, 0.0, -30000.0)
    ident = np.eye(128, dtype=np.float32)
    tri = np.zeros((3, 128, 128), np.float32)
    a = np.arange(128)[:, None]
    b = np.arange(128)[None, :]
    tri[0] = (a <= b)
    tri[1] = (a > b)
    tri[2] = 1.0
    return dict(relb=relb, amask=amask, c_ident=ident, c_tri=tri)


_CACHE = {}


def kernel(**inputs):
    inputs = {k: np.asarray(v) for k, v in inputs.items()}
    if "nc" not in _CACHE:
        _CACHE["nc"] = Builder().build()
    nc = _CACHE["nc"]
    hc = host_consts(inputs["rel_bias"].astype(np.float32))
    shared = {k: np.ascontiguousarray(v, dtype=np.float32) for k, v in inputs.items()
              if k not in ("x", "mem", "rel_bias")}
    shared.update(hc)
    in_maps = []
    for b in range(8):
        m = dict(shared)
        m["x"] = np.ascontiguousarray(inputs["x"][b], dtype=np.float32)
        m["mem"] = np.ascontiguousarray(inputs["mem"][b], dtype=np.float32)
        in_maps.append(m)
    res = run_bass_kernel_spmd(nc, in_maps, core_ids=list(range(8)))
    return np.stack([np.asarray(r["out"], dtype=np.float32) for r in res.results], axis=0)
```
